# Optimizing a Trainium2 kernel written in Bass

```python
import jax, jax.numpy as jnp
from jax import lax
import numpy as np

D_MODEL = 1024
BATCH = 8
SEQ = 2048
DEPTH = 1

CHUNK = 64
HEAD_DIM = 64
H_SB = 8
H_CH = 8
W_SB = H_SB * HEAD_DIM
W_CH = H_CH * HEAD_DIM
MIX_WIDTH = W_SB + W_CH
LOOKBACK = 8
BAND = (LOOKBACK + 1) * CHUNK
REL_CLIP = 128
Q_BLOCK = 128
D_FF = 2816
PLE_DIM = 256
EPS = 1e-6
NEG_INF = -1e30

kernel_name = "hybrid_stickbreak_chunkattn_macaron_block"


def rms_norm(x, g):
    xf = x.astype(jnp.float32)
    y = xf * lax.rsqrt(jnp.mean(xf * xf, axis=-1, keepdims=True) + EPS)
    return (y * g.astype(jnp.float32)).astype(x.dtype)


def swiglu(x, w_gate, w_up, w_down):
    return (jax.nn.silu(x @ w_gate) * (x @ w_up)) @ w_down


def split_heads(t, n_heads):
    b, s, _ = t.shape
    return t.reshape(b, s, n_heads, HEAD_DIM).transpose(0, 2, 1, 3)


def merge_heads(t):
    b, h, s, d = t.shape
    return t.transpose(0, 2, 1, 3).reshape(b, s, h * d)


def stick_breaking_attention(q, k, v):
    b, h, s, d = q.shape
    nq = s // Q_BLOCK
    scale = d ** -0.5
    q_blocks = q.reshape(b, h, nq, Q_BLOCK, d).transpose(2, 0, 1, 3, 4)
    starts = jnp.arange(nq, dtype=jnp.int32) * Q_BLOCK
    key_pos = jnp.arange(s, dtype=jnp.int32)

    def one_block(args):
        q_blk, start = args
        z = jnp.einsum('bhqd,bhkd->bhqk', q_blk, k,
                       preferred_element_type=jnp.float32) * scale
        q_pos = start + jnp.arange(Q_BLOCK, dtype=jnp.int32)
        before = key_pos[None, :] < q_pos[:, None]
        log_fail = jnp.where(before, jax.nn.log_sigmoid(-z), 0.0)
        later = lax.cumsum(log_fail, axis=3, reverse=True) - log_fail
        log_a = jax.nn.log_sigmoid(z) + later
        a = jnp.where(before, jnp.exp(jnp.where(before, log_a, 0.0)), 0.0)
        return jnp.einsum('bhqk,bhkd->bhqd', a.astype(v.dtype), v)

    out = lax.map(one_block, (q_blocks, starts))
    return out.transpose(1, 2, 0, 3, 4).reshape(b, h, s, d)


def rel_bias_index():
    i = np.arange(CHUNK)[:, None]
    j = np.arange(BAND)[None, :]
    dist = i + LOOKBACK * CHUNK - j
    return jnp.asarray(np.clip(dist, -REL_CLIP, REL_CLIP) + REL_CLIP, dtype=jnp.int32)


def chunk_band_attention(q, k, v, rel_bias):
    b, h, s, d = q.shape
    nc = s // CHUNK
    scale = d ** -0.5
    qc = q.reshape(b, h, nc, CHUNK, d)

    def band(t):
        tc = t.reshape(b, h, nc, CHUNK, d)
        tp = jnp.pad(tc, ((0, 0), (0, 0), (LOOKBACK, 0), (0, 0), (0, 0)))
        return jnp.concatenate([tp[:, :, w:w + nc] for w in range(LOOKBACK + 1)], axis=3)

    kb, vb = band(k), band(v)
    bias = rel_bias.astype(jnp.float32)[:, rel_bias_index()]
    z = jnp.einsum('bhnqd,bhnkd->bhnqk', qc, kb,
                   preferred_element_type=jnp.float32) * scale + bias[None, :, None]
    slot_chunk = jnp.arange(BAND, dtype=jnp.int32) // CHUNK
    chunk_id = jnp.arange(nc, dtype=jnp.int32)
    valid = (chunk_id[:, None] + slot_chunk[None, :] - LOOKBACK) >= 0
    z = jnp.where(valid[None, None, :, None, :], z, NEG_INF)
    prob = jax.nn.softmax(z, axis=-1)
    o = jnp.einsum('bhnqk,bhnkd->bhnqd', prob.astype(vb.dtype), vb)
    return o.reshape(b, h, s, d)


def setup_inputs(seed: int = 0) -> dict:
    key = jax.random.key(seed)
    ks = jax.random.split(key, 24)
    f32 = jnp.float32

    def w(k, shape, fan_in):
        return jax.random.normal(k, shape, f32) * (fan_in ** -0.5)

    def gain(k, n):
        return 1.0 + 0.05 * jax.random.normal(k, (DEPTH, n), f32)

    return {
        "x": jax.random.normal(ks[0], (BATCH, SEQ, D_MODEL), f32),
        "p": jax.random.normal(ks[1], (DEPTH, BATCH, SEQ, PLE_DIM), f32),
        "g_ffn1_pre": gain(ks[2], D_MODEL),
        "g_ffn1_post": gain(ks[3], D_MODEL),
        "w_ffn1_gate": w(ks[4], (DEPTH, D_MODEL, D_FF), D_MODEL),
        "w_ffn1_up": w(ks[5], (DEPTH, D_MODEL, D_FF), D_MODEL),
        "w_ffn1_down": w(ks[6], (DEPTH, D_FF, D_MODEL), D_FF),
        "g_mix_pre": gain(ks[7], D_MODEL),
        "g_mix_post": gain(ks[8], D_MODEL),
        "w_in": w(ks[9], (DEPTH, D_MODEL, 3 * MIX_WIDTH), D_MODEL),
        "g_out_sb": gain(ks[10], W_SB),
        "g_out_ch": gain(ks[11], W_CH),
        "rel_bias": 0.02 * jax.random.normal(ks[12], (DEPTH, H_CH, 2 * REL_CLIP + 1), f32),
        "w_out": w(ks[13], (DEPTH, MIX_WIDTH, D_MODEL), MIX_WIDTH),
        "g_ffn2_pre": gain(ks[14], D_MODEL),
        "g_ffn2_post": gain(ks[15], D_MODEL),
        "w_ffn2_gate": w(ks[16], (DEPTH, D_MODEL, D_FF), D_MODEL),
        "w_ffn2_up": w(ks[17], (DEPTH, D_MODEL, D_FF), D_MODEL),
        "w_ffn2_down": w(ks[18], (DEPTH, D_FF, D_MODEL), D_FF),
        "w_ple_proj": w(ks[19], (DEPTH, PLE_DIM, D_MODEL), PLE_DIM),
        "w_ple_gate": w(ks[20], (DEPTH, D_MODEL, D_MODEL), D_MODEL),
        "g_ple_post": gain(ks[21], D_MODEL),
    }


def reference(x, p, g_ffn1_pre, g_ffn1_post, w_ffn1_gate, w_ffn1_up, w_ffn1_down,
              g_mix_pre, g_mix_post, w_in, g_out_sb, g_out_ch, rel_bias, w_out,
              g_ffn2_pre, g_ffn2_post, w_ffn2_gate, w_ffn2_up, w_ffn2_down,
              w_ple_proj, w_ple_gate, g_ple_post):
    h = x
    for i in range(DEPTH):
        f = swiglu(rms_norm(h, g_ffn1_pre[i]), w_ffn1_gate[i], w_ffn1_up[i], w_ffn1_down[i])
        h = h + 0.5 * rms_norm(f, g_ffn1_post[i])

        u = rms_norm(h, g_mix_pre[i])
        qkv = u @ w_in[i]
        q_a, k_a, v_a, q_b, k_b, v_b = jnp.split(
            qkv, np.cumsum([W_SB, W_SB, W_SB, W_CH, W_CH])[:5].tolist(), axis=-1)
        o_a = stick_breaking_attention(split_heads(q_a, H_SB), split_heads(k_a, H_SB),
                                       split_heads(v_a, H_SB))
        o_b = chunk_band_attention(split_heads(q_b, H_CH), split_heads(k_b, H_CH),
                                   split_heads(v_b, H_CH), rel_bias[i])
        mixed = jnp.concatenate([rms_norm(merge_heads(o_a), g_out_sb[i]),
                                 rms_norm(merge_heads(o_b), g_out_ch[i])], axis=-1)
        h = h + rms_norm(mixed @ w_out[i], g_mix_post[i])

        f = swiglu(rms_norm(h, g_ffn2_pre[i]), w_ffn2_gate[i], w_ffn2_up[i], w_ffn2_down[i])
        h = h + 0.5 * rms_norm(f, g_ffn2_post[i])

        e = (p[i] @ w_ple_proj[i]) * jax.nn.sigmoid(h @ w_ple_gate[i])
        h = h + rms_norm(e, g_ple_post[i])
    return h
```

```python
import numpy as np
import concourse.bass as bass
import concourse.mybir as mybir
from concourse.bass_utils import run_bass_kernel_spmd

F32 = mybir.dt.float32
BF16 = mybir.dt.bfloat16
AF = mybir.ActivationFunctionType
ALU = mybir.AluOpType

ENGS = ("sync", "scalar", "vector", "gpsimd", "tensor")


class Sched:
    def __init__(self):
        self.prog = {e: [] for e in ENGS}
        self.cnt = {}
        self.known = {e: {} for e in ENGS}
        self.last_w = {}
        self.readers = {}
        self.last_x = {}

    def _emit(self, eng, fn, reads, writes, excl, sem, inc):
        toks = []
        for r in reads:
            t = self.last_w.get(r)
            if t is not None:
                toks.append(t)
        for w in writes:
            t = self.last_w.get(w)
            if t is not None:
                toks.append(t)
            toks.extend(self.readers.get(w, ()))
        for x in excl:
            t = self.last_x.get(x)
            if t is not None and not (t[2] == eng and eng == "tensor"):
                toks.append((t[0], t[1], t[2]))
        kn = self.known[eng]
        need = {}
        for (s, v, src) in toks:
            if src == eng and eng == "tensor" and s == "tensor":
                continue
            if kn.get(s, 0) < v and need.get(s, 0) < v:
                need[s] = v
        for s, v in need.items():
            kn[s] = v
        self.cnt[sem] = self.cnt.get(sem, 0) + inc
        tok = (sem, self.cnt[sem], eng if sem == eng else "dma:" + sem)
        self.prog[eng].append((tuple(need.items()), fn, sem, inc))
        for r in reads:
            self.readers.setdefault(r, []).append(tok)
        for w in writes:
            self.last_w[w] = tok
            self.readers[w] = []
        for x in excl:
            self.last_x[x] = (tok[0], tok[1], eng)
        return tok

    def op(self, eng, fn, reads=(), writes=(), excl=()):
        return self._emit(eng, fn, reads, writes, excl, eng, 1)

    def dma(self, eng, fns, sem, reads=(), writes=()):
        return self._emit(eng, list(fns), reads, writes, (), sem, 16 * len(fns))

    def finish(self, nc, block, final_waits=()):
        names = set(ENGS) | {s for e in ENGS for (_, _, s, _) in self.prog[e] if s is not None}
        sems = {}
        import contextlib
        self._stack = contextlib.ExitStack()
        for n in sorted(names):
            sems[n] = self._stack.enter_context(nc.semaphore("s_" + n))
        finals = [(s, self.cnt[s]) for s in final_waits]
        allfin = [(e, self.cnt.get(e, 0)) for e in ENGS if self.cnt.get(e, 0) > 0]

        def make(engname):
            def f(e):
                for (waits, fn, s, inc) in self.prog[engname]:
                    for (ws, wv) in waits:
                        e.wait_ge(sems[ws], wv)
                    if fn is None:
                        continue
                    if isinstance(fn, list):
                        for f1 in fn:
                            f1(e).then_inc(sems[s], 16)
                    else:
                        fn(e).then_inc(sems[s], inc)
                if engname == "sync":
                    for (s, v) in finals + allfin:
                        e.wait_ge(sems[s], v)
            return f

        block.sync(make("sync"))
        block.scalar(make("scalar"))
        block.vector(make("vector"))
        block.gpsimd(make("gpsimd"))
        block.tensor(make("tensor"))

    def barrier(self):
        for eng in ENGS:
            kn = self.known[eng]
            need = {s: v for s, v in self.cnt.items() if kn.get(s, 0) < v}
            for s, v in need.items():
                kn[s] = v
            if need:
                self.prog[eng].append((tuple(need.items()), None, None, 0))


T = 2048
D = 1024
NTB = 16
FF = 2816
NFC = 22
EPS = 1e-6
NEG = -30000.0

G_FFN1_PRE, G_FFN1_POST, G_MIX_PRE, G_MIX_POST, G_FFN2_PRE, G_FFN2_POST, G_PLE = range(7)


class Arena:
    def __init__(self, ap, nbytes):
        self.ap = ap
        self.nbytes = nbytes
        self.off = 0

    def mark(self):
        return self.off

    def reset(self, m):
        self.off = m

    def alloc(self, shape, dt):
        n = int(np.prod(shape))
        nb = n * (4 if dt == F32 else 2)
        nb_al = (nb + 31) // 32 * 32
        assert self.off + nb_al <= self.nbytes, (self.off, nb_al, self.nbytes)
        a = self.ap[:, self.off // 2: (self.off + nb) // 2]
        self.off += nb_al
        if dt == F32:
            a = a.bitcast(F32)
        if len(shape) == 2:
            a = a.rearrange("p (a b) -> p a b", a=shape[0])
        elif len(shape) == 3:
            a = a.rearrange("p (a b c) -> p a b c", a=shape[0], b=shape[1])
        return a


ARENA_BYTES = 212800


def build(debug=False):
    nc = bass.Bass("TRN2", target_bir_lowering=False)
    dt_in = lambda n, s: nc.dram_tensor(n, s, F32, kind="ExternalInput").ap()
    x_d = dt_in("x", [T, D])
    p_d = dt_in("p", [T, 256])
    g7_d = dt_in("g7", [7, D])
    gout_d = dt_in("gout", [128, 8])
    cst_d = dt_in("cst", [128, 6 * 128 + 512])
    biasq_d = dt_in("biasq", [128, 8 * 256])
    bconst_d = dt_in("bconst", [128, 8])
    wg_d = [dt_in("wg1", [NFC, 128, 1024]), dt_in("wg2", [NFC, 128, 1024])]
    wu_d = [dt_in("wu1", [NFC, 128, 1024]), dt_in("wu2", [NFC, 128, 1024])]
    wd_d = [dt_in("wd1", [128, NFC * 1024]), dt_in("wd2", [128, NFC * 1024])]
    win_d = dt_in("win", [8, 128, 3 * 1024])
    wout_d = dt_in("wout", [128, 8 * 1024])
    wgate_d = dt_in("wgate", [128, 8 * 1024])
    wple_d = dt_in("wple", [128, 2 * 1024])
    out_d = nc.dram_tensor("out", [T, D], F32, kind="ExternalOutput").ap()

    S = Sched()
    with (
        nc.sbuf_tensor("arena", [128, ARENA_BYTES // 2], BF16) as arena_t,
        nc.psum_tensor("ps", [128, 4096], F32) as ps,
        nc.Block() as block,
    ):
        A = Arena(arena_t, ARENA_BYTES)
        h = A.alloc([NTB, D], F32)
        cst = A.alloc([6, 128], BF16)
        ident, tri, ctri, ones = cst[:, 0, :], cst[:, 1, :], cst[:, 2, :], cst[:, 3, :]
        ones_h = [cst[:, 4, :], cst[:, 5, :]]
        sbmask = A.alloc([512], BF16)
        gbuf = A.alloc([2, D], F32)
        gout = A.alloc([8], F32)
        bconst = A.alloc([8], F32)
        stat = A.alloc([256], F32)
        junk = A.alloc([D], BF16)
        tmpA = A.alloc([D], F32)
        tmpB = A.alloc([D], F32)
        xn_tm = A.alloc([2, D], BF16)
        base_mark = A.mark()

        def bank(b, n=1):
            return ps[:, b * 512:(b + n) * 512]

        def bank_bf(b):
            return ps[:, b * 512:(b + 1) * 512].bitcast(BF16)

        PB = lambda *bs: [("ps", b) for b in bs]
        TP = lambda ro: ({"tile_position": (0, ro)} if ro else {})

        stat_ctr = [0]

        def stat_cols(n):
            c = stat_ctr[0]
            if c + n > 112:
                c = 0
            stat_ctr[0] = c + n
            return c

        S.dma("gpsimd", [lambda e: e.dma_start(out=cst.rearrange("p a b -> p (a b)"), in_=cst_d[:, 0:768]),
                         lambda e: e.dma_start(out=sbmask, in_=cst_d[:, 768:1280])],
              "cst", writes=[("cst",)])
        S.dma("sync", [lambda e: e.dma_start(out=gout, in_=gout_d), lambda e: e.dma_start(out=bconst, in_=bconst_d)], "gout", writes=[("gout",), ("bconst",)])

        def load_gains(i0, i1, half1):
            S.dma("sync", [lambda e: e.dma_start(out=gbuf[:, 0, :], in_=g7_d[i0:i0 + 1, :].broadcast_to([128, D]))],
                  "g0", writes=[("gbuf", 0)])
            if i1 is not None:
                S.dma("sync", [lambda e: e.dma_start(out=gbuf[:, 1, :], in_=g7_d[i1:i1 + 1, :].broadcast_to([128, D]))],
                      "g1", writes=[("gbuf", 1)])
                if half1:
                    S.op("vector", lambda e: e.tensor_scalar(out=gbuf[:, 1, :], in0=gbuf[:, 1, :], scalar1=0.5, scalar2=None, op0=ALU.mult),
                         reads=[("gbuf", 1)], writes=[("gbuf", 1)])

        def rstd_from_ss(ss_ap, r_ap, n, inv_n, rkeys_in, wkeys_out, use_lnexp=False):
            if not use_lnexp:
                S.op("scalar", lambda e: e.activation(out=r_ap, in_=ss_ap, func=AF.Sqrt, scale=inv_n, bias=EPS),
                     reads=rkeys_in, writes=wkeys_out)
                S.op("vector", lambda e: e.reciprocal(out=r_ap, in_=r_ap), reads=wkeys_out, writes=wkeys_out)
            else:
                S.op("vector", lambda e: e.tensor_scalar(out=r_ap, in0=ss_ap, scalar1=inv_n, scalar2=EPS, op0=ALU.mult, op1=ALU.add),
                     reads=rkeys_in, writes=wkeys_out)
                S.op("scalar", lambda e: e.activation(out=r_ap, in_=r_ap, func=AF.Ln), reads=wkeys_out, writes=wkeys_out)
                S.op("scalar", lambda e: e.activation(out=r_ap, in_=r_ap, func=AF.Exp, scale=-0.5), reads=wkeys_out, writes=wkeys_out)

        TRB = 6
        xn_ctr = [0]

        def sq_stat(tb, col):
            S.op("scalar", lambda e: e.activation(out=junk, in_=h[:, tb, :], func=AF.Square, accum_out=stat[:, col:col + 1]),
                 reads=[("h", tb)], writes=[("junk",), ("stat", col)])

        def norm_stats(tbs):
            n = len(tbs)
            c0 = stat_cols(2 * n)
            for i, tb in enumerate(tbs):
                S.op("scalar", lambda e, tb=tb, i=i: e.activation(out=junk, in_=h[:, tb, :], func=AF.Square, accum_out=stat[:, c0 + i:c0 + i + 1]),
                     reads=[("h", tb)], writes=[("junk",), ("stat", c0 + i)])
            rstd_from_ss(stat[:, c0:c0 + n], stat[:, c0 + n:c0 + 2 * n], n, 1.0 / D,
                         [("stat", c0 + i) for i in range(n)], [("stat", c0 + n + i) for i in range(n)])
            return c0

        def nx_a(tb, i, c0, n):
            sl = xn_ctr[0] % 2
            xn_ctr[0] += 1
            S.op("vector", lambda e: e.scalar_tensor_tensor(
                out=xn_tm[:, sl, :], in0=h[:, tb, :], scalar=stat[:, c0 + n + i:c0 + n + i + 1], in1=gbuf[:, 0, :],
                op0=ALU.mult, op1=ALU.mult),
                reads=[("h", tb), ("stat", c0 + n + i), ("gbuf", 0)], writes=[("xn", sl)])
            return sl

        def norm_xpose_one(tb, i, c0, n, dstT, dst_key, col, trb, evac_eng):
            sl = nx_a(tb, i, c0, n)
            nx_b(sl, dstT, dst_key, col, trb, evac_eng)

        def nx_b(sl, dstT, dst_key, col, trb, evac_eng):
            tps = bank_bf(trb)

            def tr(e):
                ins = None
                for kc in range(8):
                    ins = e.transpose(out=tps[:, kc * 128:(kc + 1) * 128], in_=xn_tm[:, sl, kc * 128:(kc + 1) * 128], identity=ident)
                return ins
            S.op("tensor", tr, reads=[("xn", sl), ("cst",)], excl=PB(trb))
            if evac_eng == "scalar":
                S.op("scalar", lambda e: e.activation(out=dstT[:, :, col:col + 128], in_=tps.rearrange("p (k t) -> p k t", k=8), func=AF.Copy),
                     writes=[(dst_key, col // 128)], excl=PB(trb))
            else:
                S.op("vector", lambda e: e.tensor_copy(out=dstT[:, :, col:col + 128], in_=tps.rearrange("p (k t) -> p k t", k=8)),
                     writes=[(dst_key, col // 128)], excl=PB(trb))

        def norm_transpose(tbs, dstT, dst_key, col_of, trbs=(6, 7), batched=True, c0=None):
            n = len(tbs)
            if batched:
                if c0 is None:
                    c0 = norm_stats(tbs)
                for i, tb in enumerate(tbs):
                    norm_xpose_one(tb, i, c0, n, dstT, dst_key, col_of(tb), trbs[i % len(trbs)], "scalar" if i % 2 == 0 else "vector")
            else:
                for i, tb in enumerate(tbs):
                    c1 = norm_stats([tb])
                    norm_xpose_one(tb, 0, c1, 1, dstT, dst_key, col_of(tb), trbs[i % len(trbs)], "scalar" if i % 2 == 0 else "vector")

        def post_norm_residual(src_ap, src_reads, src_excl, tb, gslot, use_lnexp=False, out_ap=None, out_key=None):
            c = stat_cols(2)
            S.op("scalar", lambda e: e.activation(out=junk, in_=src_ap, func=AF.Square, accum_out=stat[:, c:c + 1]),
                 reads=src_reads, writes=[("junk",), ("stat", c)], excl=src_excl)
            rstd_from_ss(stat[:, c:c + 1], stat[:, c + 1:c + 2], 1, 1.0 / D, [("stat", c)], [("stat", c + 1)], use_lnexp)
            S.op("vector", lambda e: e.scalar_tensor_tensor(out=tmpA, in0=src_ap, scalar=stat[:, c + 1:c + 2], in1=gbuf[:, gslot, :],
                                                            op0=ALU.mult, op1=ALU.mult),
                 reads=list(src_reads) + [("stat", c + 1), ("gbuf", gslot)], writes=[("tmpA",)], excl=src_excl)
            if out_ap is None:
                S.op("vector", lambda e: e.tensor_tensor(out=h[:, tb, :], in0=h[:, tb, :], in1=tmpA, op=ALU.add),
                     reads=[("tmpA",), ("h", tb)], writes=[("h", tb)])
            else:
                S.op("vector", lambda e: e.tensor_tensor(out=out_ap, in0=h[:, tb, :], in1=tmpA, op=ALU.add),
                     reads=[("tmpA",), ("h", tb)], writes=[out_key])

        wgu_ctr = [0]

        A.reset(base_mark)
        xT = A.alloc([8, 1024], BF16)
        hid = A.alloc([NFC, 1024], BF16)
        wd = A.alloc([NFC, 1024], BF16)
        wgu = A.alloc([2, 2, 1024], BF16)
        sg = A.alloc([2, 512], BF16)
        A.reset(base_mark)
        pre = {}

        def load_wgu(li, fc):
            sl = wgu_ctr[0] % 2
            wgu_ctr[0] += 1
            S.dma("gpsimd", [lambda e: e.dma_start(out=wgu[:, sl, 0, :], in_=wg_d[li][fc]),
                             lambda e: e.dma_start(out=wgu[:, sl, 1, :], in_=wu_d[li][fc])],
                  "wgu%d" % sl, writes=[("wgu", sl)])
            return sl

        def ffn_prefetch(li, gpre, gpost):
            load_gains(gpre, gpost, True)
            pre["gains%d" % li] = True
            rstd_from_ss(stat[:, 112:120], stat[:, 120:128], 8, 1.0 / D,
                         [("stat", 112 + i) for i in range(8)], [("stat", 120 + i) for i in range(8)])
            pre["stats%d" % li] = 112
            pre["wgu%d" % li] = [load_wgu(li, 0), load_wgu(li, 1)]

        def ple_prefetch():
            for j in range(0, 8, 2):
                S.dma("gpsimd", [lambda e, j=j: e.dma_start(out=xT[:, j:j + 2, :], in_=wgate_d[:, j * 1024:(j + 2) * 1024].rearrange("p (a b) -> p a b", a=2))],
                      "wgate%d" % (j // 2), writes=[("wgate", j), ("wgate", j + 1)] + [("xT", i) for i in range(8)])
            S.dma("gpsimd", [lambda e: e.dma_start(out=wgu[:, 0, :, :], in_=wple_d.rearrange("p (a b) -> p a b", a=2))], "wple",
                  writes=[("wple",), ("wgu", 0)])
            load_gains(G_PLE, None, False)

        def ffn(li, gpre, gpost):
            S.barrier()
            def load_x(tb):
                S.dma("sync", [lambda e: e.dma_start(out=h[:, tb, :], in_=x_d[tb * 128:(tb + 1) * 128, :])],
                      "x%d" % tb, writes=[("h", tb)])
            if li == 0:
                for tb in range(8):
                    load_x(tb)
            if not pre.get("gains%d" % li):
                load_gains(gpre, gpost, True)
            if li == 0:
                for tb in range(8, NTB):
                    load_x(tb)

            def load_wd(j):
                S.dma("gpsimd", [lambda e: e.dma_start(out=wd[:, j:j + 2, :].rearrange("p a b -> p (a b)"),
                                                      in_=wd_d[li][:, j * 1024:(j + 2) * 1024])],
                      "wd%d" % (j // 2), writes=[("wd", j), ("wd", j + 1)])
            it = 0
            for hf in range(2):
                tbs = list(range(8 * hf, 8 * hf + 8))
                if hf == 0:
                    if "stats%d" % li in pre:
                        c0p = pre["stats%d" % li]
                        for i, tb in enumerate(tbs):
                            norm_xpose_one(tb, i, c0p, 8, xT, "xT", (tb % 8) * 128, 6 + i % 2, "scalar" if i % 2 == 0 else "vector")
                    elif li == 0:
                        cs = stat_cols(16)
                        sls = {}
                        for i in range(10):
                            if i < 8:
                                sq_stat(i, cs + i)
                            if 1 <= i <= 8:
                                j = i - 1
                                rstd_from_ss(stat[:, cs + j:cs + j + 1], stat[:, cs + 8 + j:cs + 9 + j], 1, 1.0 / D, [("stat", cs + j)], [("stat", cs + 8 + j)])
                                sls[j] = nx_a(j, j, cs, 8)
                            if i >= 2:
                                j = i - 2
                                nx_b(sls[j], xT, "xT", j * 128, 6 + j % 2, "vector" if j < 6 else "scalar")
                    else:
                        norm_transpose(tbs, xT, "xT", lambda tb: (tb % 8) * 128)
                for fc in range(NFC):
                    if hf == 0 and 2 <= fc < 13:
                        load_wd(2 * (fc - 2))
                    if hf == 0 and fc < 2 and pre.get("wgu%d" % li):
                        sl = pre["wgu%d" % li][fc]
                    else:
                        sl = load_wgu(li, fc)
                    for tt in range(2):
                        gb, ub = it % 2, 2 + it % 2
                        sgs = it % 2
                        it += 1

                        def mm(e, sl=sl, tt=tt, gb=gb, ub=ub):
                            ins = None
                            for m, bk in ((0, gb), (1, ub)):
                                for kc in range(8):
                                    ins = e.matmul(bank(bk), lhsT=wgu[:, sl, m, kc * 128:(kc + 1) * 128],
                                                   rhs=xT[:, kc, tt * 512:(tt + 1) * 512], start=(kc == 0), stop=(kc == 7))
                            return ins
                        S.op("tensor", mm, reads=[("wgu", sl)] + [("xT", tt * 4 + q) for q in range(4)], excl=PB(gb, ub))
                        S.op("scalar", lambda e, gb=gb, sgs=sgs: e.activation(out=sg[:, sgs, :], in_=bank(gb), func=AF.Silu),
                             writes=[("sg", sgs)], excl=PB(gb))
                        S.op("vector", lambda e, ub=ub, sgs=sgs, fc=fc, tt=tt: e.tensor_tensor(
                            out=hid[:, fc, tt * 512:(tt + 1) * 512], in0=bank(ub), in1=sg[:, sgs, :], op=ALU.mult),
                            reads=[("sg", sgs)], writes=[("hid", fc, tt)], excl=PB(ub))
                if hf == 0:
                    nx_c0 = norm_stats(list(range(8, 16)))
                if hf == 1 and li == 1:
                    ple_prefetch()
                if hf == 1 and li == 0:
                    S.dma("sync", [lambda e: e.dma_start(out=gbuf[:, 0, :], in_=g7_d[G_MIX_PRE:G_MIX_PRE + 1, :].broadcast_to([128, D]))],
                          "g0", writes=[("gbuf", 0)])
                    for tb_ in range(8):
                        sq_stat(tb_, 128 + tb_)
                    rstd_from_ss(stat[:, 128:136], stat[:, 144:152], 8, 1.0 / D,
                                 [("stat", 128 + i) for i in range(8)], [("stat", 144 + i) for i in range(8)])
                for tbl in range(8):
                    tb = 8 * hf + tbl
                    fb = 4 if tbl % 2 == 0 else 6

                    def mmd(e, tbl=tbl, fb=fb):
                        ins = None
                        for dh in range(2):
                            for fc in range(NFC):
                                ins = e.matmul(bank(fb + dh), lhsT=hid[:, fc, tbl * 128:(tbl + 1) * 128],
                                               rhs=wd[:, fc, dh * 512:(dh + 1) * 512], start=(fc == 0), stop=(fc == NFC - 1))
                        return ins
                    if hf == 0:
                        nsl = nx_a(8 + tbl, tbl, nx_c0, 8)
                    elif li == 0:
                        nsl = nx_a(tbl, tbl, 128, 16)
                    S.op("tensor", mmd, reads=[("hid", fc, tbl // 4) for fc in range(NFC)] + [("wd", fc) for fc in range(NFC)],
                         excl=PB(fb, fb + 1))
                    if hf == 0 or li == 0:
                        nx_b(nsl, xT, "xT", tbl * 128, tbl % 2, "scalar" if tbl % 2 == 0 else "vector")
                    post_norm_residual(bank(fb, 2), [], PB(fb, fb + 1), tb, 1)
                    if li == 0 and hf == 1:
                        sq_stat(tb, 128 + tb)

        def attention_prefetch():
            S.dma("sync", [lambda e: e.dma_start(out=gbuf[:, 1, :], in_=g7_d[G_MIX_POST:G_MIX_POST + 1, :].broadcast_to([128, D]))],
                  "g1", writes=[("gbuf", 1)])
            rstd_from_ss(stat[:, 136:144], stat[:, 152:160], 8, 1.0 / D,
                         [("stat", 136 + i) for i in range(8)], [("stat", 152 + i) for i in range(8)])
            pre["attn_stats"] = 128

        def attention():
            S.barrier()
            A.reset(base_mark)
            uTh = [A.alloc([8, 1024], BF16), A.alloc([8, 1024], BF16)]

            def uTs(kc, c0_, c1_):
                hf_ = c0_ // 1024
                return uTh[hf_][:, kc, c0_ - 1024 * hf_:c1_ - 1024 * hf_]
            oT = A.alloc([8, T], BF16)
            ep_mark = A.mark()
            qT = A.alloc([2, T], BF16)
            kT = A.alloc([T], BF16)
            vv = A.alloc([2, NTB, 128], BF16)
            wqkv = A.alloc([3, 1024], BF16)
            sq = A.alloc([T], BF16)
            NW = 3
            wk_mark = A.mark()
            e_b = A.alloc([NW, 2, 512], BF16)
            sp_b = A.alloc([NW, 2, 512], BF16)
            g_b = A.alloc([3, 2, 512], BF16)
            a_b = A.alloc([3, 2, 512], BF16)
            A.reset(wk_mark)
            E_b = A.alloc([2, 640], BF16)
            E2_b = A.alloc([3, 640], BF16)
            biasb = A.alloc([8, 640], BF16)
            ssacc = stat[:, 192:224]
            S.op("vector", lambda e: e.memset(ssacc, 0.0), writes=[("ssacc",)])
            S.op("vector", lambda e: e.memset(qT[64:128, 0, :], 0.0), writes=[("qT", i) for i in range(4)])
            S.op("vector", lambda e: e.memset(qT[0:64, 1, :], 0.0), writes=[("qT", i) for i in range(4)])
            S.op("vector", lambda e: e.memset(vv[:, 0, :, 64:128], 0.0), writes=[("vv", i) for i in range(4)])
            S.op("vector", lambda e: e.memset(vv[:, 1, :, 0:64], 0.0), writes=[("vv", i) for i in range(4)])
            for i_, tb_ in enumerate(range(8, NTB)):
                norm_xpose_one(tb_, tb_, 128, 16, uTh[1], "uT", (tb_ % 8) * 128, 6 + i_ % 2, "scalar" if i_ % 2 == 0 else "vector")
            uT_all = [("uT", i) for i in range(NTB)]
            wout = uTh[0]

            qT1 = oT[:, 4:6, :]
            vv1 = oT[:, 6:8, :].rearrange("p s (t c) -> p s t c", c=128)
            kT1 = xn_tm.rearrange("p a b -> p (a b)")
            gb0 = gbuf[:, 0, :].bitcast(BF16)
            sets = [
                dict(q=qT, k=kT, v=vv, w=[wqkv[:, 0, :], wqkv[:, 1, :], wqkv[:, 2, :]], kq="qT", kk="kT", kv="vv",
                     kw=[("wqkv",)], sem="win"),
                dict(q=qT1, k=kT1, v=vv1, w=[junk, gb0[:, 0:1024], gb0[:, 1024:2048]], kq="qT1", kk="kT1", kv="vv1",
                     kw=[("junk",), ("gbuf", 0), ("xn", 0), ("xn", 1)], sem="win1"),
            ]
            st1 = sets[1]
            S.op("vector", lambda e: e.memset(st1["q"][64:128, 0, :], 0.0), writes=[("qT1", i) for i in range(4)])
            S.op("vector", lambda e: e.memset(st1["q"][0:64, 1, :], 0.0), writes=[("qT1", i) for i in range(4)])
            S.op("vector", lambda e: e.memset(st1["v"][:, 0, :, 64:128], 0.0), writes=[("vv1", i) for i in range(4)])
            S.op("vector", lambda e: e.memset(st1["v"][:, 1, :, 0:64], 0.0), writes=[("vv1", i) for i in range(4)])

            def issue_w(pair, st):
                S.dma("gpsimd", [lambda e, m=m: e.dma_start(out=st["w"][m], in_=win_d[pair][:, m * 1024:(m + 1) * 1024]) for m in range(3)],
                      st["sem"], writes=st["kw"])

            def proj_closures(pair, st, overlapped):
                out = []
                units = [(m, t) for m in (0, 1) for t in range(4)] + [(2, t) for t in range(4)]
                for ui, (m, t) in enumerate(units):
                    if overlapped:
                        bk = 7
                    else:
                        bk = (t % 2) if m < 2 else 2 + t % 2
                    if m < 2:
                        mm = [(None, kc) for kc in range(8)]
                        splits = [3, 3, 2]
                        rkeys = st["kw"] + uT_all[t * 4:t * 4 + 4]
                    else:
                        mm = [(j, kc) for j in range(4) for kc in range(8)]
                        splits = [11, 11, 10]
                        rkeys = st["kw"] + uT_all[t * 4:t * 4 + 4]
                    if not overlapped:
                        splits = [len(mm)]
                    pos = 0
                    for si, n_ in enumerate(splits):
                        part = mm[pos:pos + n_]
                        pos += n_
                        last = (si == len(splits) - 1)

                        def emit(part=part, last=last, m=m, t=t, bk=bk, rkeys=rkeys):
                            def f(e):
                                ins = None
                                for (j, kc) in part:
                                    if m < 2:
                                        ins = e.matmul(bank(bk), lhsT=st["w"][m][:, kc * 128:(kc + 1) * 128], rhs=uTs(kc, t * 512, (t + 1) * 512),
                                                       start=(kc == 0), stop=(kc == 7))
                                    else:
                                        tb = 4 * t + j
                                        ins = e.matmul(bank(bk)[:, j * 128:(j + 1) * 128], lhsT=uTs(kc, tb * 128, (tb + 1) * 128),
                                                       rhs=st["w"][2][:, kc * 128:(kc + 1) * 128], start=(kc == 0), stop=(kc == 7))
                                return ins
                            S.op("tensor", f, reads=rkeys, excl=PB(bk))
                            if not last:
                                return
                            if m == 0:
                                for hh_, (lo, hi) in enumerate(((0, 64), (64, 128))):
                                    if overlapped:
                                        S.op("vector", lambda e, hh_=hh_, lo=lo, hi=hi: e.tensor_scalar(
                                            out=st["q"][lo:hi, hh_, t * 512:(t + 1) * 512], in0=bank(bk)[lo:hi, :], scalar1=0.125, scalar2=None, op0=ALU.mult),
                                            writes=[(st["kq"], t)], excl=PB(bk))
                                    else:
                                        S.op("scalar", lambda e, hh_=hh_, lo=lo, hi=hi: e.activation(
                                            out=st["q"][lo:hi, hh_, t * 512:(t + 1) * 512], in_=bank(bk)[lo:hi, :], func=AF.Copy, scale=0.125),
                                            writes=[(st["kq"], t)], excl=PB(bk))
                            elif m == 1:
                                S.op("vector", lambda e: e.tensor_copy(out=st["k"][:, t * 512:(t + 1) * 512], in_=bank(bk)),
                                     writes=[(st["kk"], t)], excl=PB(bk))
                            else:
                                src = bank(bk).rearrange("p (a b) -> p a b", a=4)
                                if overlapped:
                                    S.op("vector", lambda e: e.tensor_copy(out=st["v"][:, 0, 4 * t:4 * t + 4, 0:64], in_=src[:, :, 0:64]),
                                         writes=[(st["kv"], t)], excl=PB(bk))
                                else:
                                    S.op("scalar", lambda e: e.activation(out=st["v"][:, 0, 4 * t:4 * t + 4, 0:64], in_=src[:, :, 0:64], func=AF.Copy),
                                         writes=[(st["kv"], t)], excl=PB(bk))
                                S.op("vector", lambda e: e.tensor_copy(out=st["v"][:, 1, 4 * t:4 * t + 4, 64:128], in_=src[:, :, 64:128]),
                                     writes=[(st["kv"], t)], excl=PB(bk))
                        out.append(emit)
                return out

            def ss_pair(grp):
                def mms(e):
                    ins = None
                    for tb in range(NTB):
                        ins = e.matmul(bank(0)[:, tb:tb + 1], lhsT=sq[:, tb * 128:(tb + 1) * 128], rhs=ones[:, 0:1], start=True, stop=True)
                    return ins
                S.op("tensor", mms, reads=[("sq", i) for i in range(4)] + [("cst",)], excl=PB(0))
                S.op("vector", lambda e: e.tensor_tensor(out=ssacc[:, grp * 16:(grp + 1) * 16], in0=bank(0)[:, 0:16],
                                                         in1=ssacc[:, grp * 16:(grp + 1) * 16], op=ALU.add),
                     reads=[("ssacc",)], writes=[("ssacc",)], excl=PB(0))

            def keys_of(st):
                return ([(st["kq"], i) for i in range(4)] + [(st["kk"], i) for i in range(4)], [(st["kv"], i) for i in range(4)])

            issue_w(0, sets[0])
            for c in proj_closures(0, sets[0], False):
                c()
            for pair in range(4):
                st = sets[pair % 2]
                nst = sets[(pair + 1) % 2]
                issue_w(pair + 1, nst)
                extra = proj_closures(pair + 1, nst, True)
                qk_all, vv_all = keys_of(st)
                sb_attention(pair, st["q"], st["k"], st["v"], oT, sq, e_b, sp_b, g_b, a_b, NW, qk_all, vv_all, extra)
                ss_pair(0)
            for pair in range(4, 8):
                st = sets[0]
                if pair == 4:
                    S.barrier()
                    S.dma("gpsimd", [lambda e: e.dma_start(out=biasb[:, :, 384:640], in_=biasq_d.rearrange("p (a b) -> p a b", a=8))], "biasq", writes=[("biasb",)])
                    for hh in range(8):
                        S.op("vector", lambda e, hh=hh: e.tensor_scalar(out=biasb[:, hh, 0:384], in0=sbmask[:, 0:384], scalar1=0.0, scalar2=bconst[:, hh:hh + 1],
                                                                       op0=ALU.mult, op1=ALU.add),
                             reads=[("bconst",), ("cst",), ("biasb",)], writes=[("biasb",)])
                    for hh in range(8):
                        S.op("vector", lambda e, hh=hh: e.memset(biasb[64:128, hh, 512:576], NEG), reads=[("biasb",)], writes=[("biasb",)])
                        S.op("vector", lambda e, hh=hh: e.memset(biasb[0:64, hh, 64:128], NEG), reads=[("biasb",)], writes=[("biasb",)])
                    S.op("scalar", lambda e: e.activation(out=biasb.rearrange("p a b -> p (a b)"), in_=biasb.rearrange("p a b -> p (a b)"), func=AF.Exp),
                         reads=[("biasb",)], writes=[("biasb",)])
                if pair == 7:
                    for j in range(0, 8, 2):
                        S.dma("gpsimd", [lambda e, j=j: e.dma_start(out=wout[:, j:j + 2, :],
                                                                   in_=wout_d[:, j * 1024:(j + 2) * 1024].rearrange("p (a b) -> p a b", a=2))],
                              "wout%d" % (j // 2), writes=[("wout", j), ("wout", j + 1)] + uT_all)
                qk_all, vv_all = keys_of(st)
                tail = []
                if pair < 7:
                    issue_w(pair + 1, st)
                    tail = proj_closures(pair + 1, st, False)
                chunk_attention(pair, st["q"], st["k"], st["v"], oT, sq, E_b, E2_b, biasb, qk_all, vv_all, tail)
                ss_pair(1)

            rg = stat[:, 224:256]
            rstd_from_ss(ssacc, rg, 32, 1.0 / 512, [("ssacc",)], [("rg",)])
            S.barrier()
            A.reset(ep_mark)
            tA2 = A.alloc([2, D], F32)
            tB2 = A.alloc([4, D], F32)
            jk2 = A.alloc([2, D], BF16)
            ecol = {}

            def e0(tb):
                pa = 0 if tb % 2 == 0 else 4

                def mmo(e):
                    ins = None
                    for g in range(2):
                        for dh in range(2):
                            for t in range(4):
                                ins = e.matmul(bank(pa + 2 * g + dh), lhsT=oT[:, 4 * g + t, tb * 128:(tb + 1) * 128],
                                               rhs=wout[:, 4 * g + t, dh * 512:(dh + 1) * 512], start=(t == 0), stop=(t == 3))
                    return ins
                S.op("tensor", mmo, reads=[("oT", t, tb // 4) for t in range(8)] + [("wout", t) for t in range(8)],
                     excl=PB(pa, pa + 1, pa + 2, pa + 3))

            def e1(tb):
                pa = 0 if tb % 2 == 0 else 4
                s4 = tb % 4
                S.op("scalar", lambda e: e.activation(out=tB2[:, s4, :], in_=bank(pa, 2), func=AF.Copy, scale=rg[:, tb:tb + 1]),
                     reads=[("rg",)], writes=[("tB2", s4)], excl=PB(pa, pa + 1))

            def e2(tb):
                pa = 0 if tb % 2 == 0 else 4
                s4 = tb % 4
                S.op("vector", lambda e: e.scalar_tensor_tensor(out=tB2[:, s4, :], in0=bank(pa + 2, 2), scalar=rg[:, 16 + tb:17 + tb], in1=tB2[:, s4, :],
                                                                op0=ALU.mult, op1=ALU.add),
                     reads=[("rg",), ("tB2", s4)], writes=[("tB2", s4)], excl=PB(pa + 2, pa + 3))

            def e3(tb):
                sl, s4 = tb % 2, tb % 4
                c = stat_cols(2)
                ecol[tb] = c
                S.op("scalar", lambda e: e.activation(out=jk2[:, sl, :], in_=tB2[:, s4, :], func=AF.Square, accum_out=stat[:, c:c + 1]),
                     reads=[("tB2", s4)], writes=[("jk2", sl), ("stat", c)])
                S.op("scalar", lambda e: e.activation(out=stat[:, c + 1:c + 2], in_=stat[:, c:c + 1], func=AF.Sqrt, scale=1.0 / D, bias=EPS),
                     reads=[("stat", c)], writes=[("stat", c + 1)])

            def e4(tb):
                sl, s4 = tb % 2, tb % 4
                c = ecol[tb]
                S.op("vector", lambda e: e.reciprocal(out=stat[:, c + 1:c + 2], in_=stat[:, c + 1:c + 2]), reads=[("stat", c + 1)], writes=[("stat", c + 1)])
                S.op("vector", lambda e: e.scalar_tensor_tensor(out=tA2[:, sl, :], in0=tB2[:, s4, :], scalar=stat[:, c + 1:c + 2], in1=gbuf[:, 1, :],
                                                                op0=ALU.mult, op1=ALU.mult),
                     reads=[("tB2", s4), ("stat", c + 1), ("gbuf", 1)], writes=[("tA2", sl)])
                S.op("vector", lambda e: e.tensor_tensor(out=h[:, tb, :], in0=h[:, tb, :], in1=tA2[:, sl, :], op=ALU.add),
                     reads=[("tA2", sl), ("h", tb)], writes=[("h", tb)])
                if tb < 8:
                    sq_stat(tb, 112 + tb)

            est = [e0, e1, e2, e3, e4]
            for t in range(NTB + len(est) - 1):
                for d in reversed(range(len(est))):
                    if 0 <= t - d < NTB:
                        est[d](t - d)

        def sb_attention(pair, qT, kT, vv, oT, sq, e_b, sp_b, g_b, a_b, NW, qk_all, vv_all, extra):
            lanes = [[], []]
            for lane, Qs in ((0, (3, 0)), (1, (2, 1))):
                for Q in Qs:
                    bs = list(range(4 * Q + 3, -1, -1))
                    for bi, b in enumerate(bs):
                        r = b - 4 * Q
                        c0 = 128 * r if r >= 0 else 0
                        lanes[lane].append(dict(Q=Q, b=b, c0=c0, n=512 - c0, diag=(r >= 0), first=(bi == 0), last=(bi == len(bs) - 1),
                                                cb=2 + 2 * lane, lane=lane, lane_first=(Q == Qs[0])))
            items = []
            for k in range(max(len(lanes[0]), len(lanes[1]))):
                for lane in range(2):
                    if k < len(lanes[lane]):
                        items.append(lanes[lane][k])
            N = len(items)
            for i, it in enumerate(items):
                it["w"] = i % NW
                it["g"] = i % 3
            deferred = {}
            z2 = ps[:, 0:1024].rearrange("p (s c) -> p s c", s=2)

            def C2(it):
                return ps[:, it["cb"] * 512:(it["cb"] + 2) * 512].rearrange("p (s c) -> p s c", s=2)

            def P1(it):
                def f(e):
                    b, Q, c0, n = it["b"], it["Q"], it["c0"], it["n"]
                    ins = None
                    for s_ in range(2):
                        ins = e.matmul(bank(s_)[:, 0:n], lhsT=kT[:, b * 128:(b + 1) * 128],
                                       rhs=qT[:, s_, Q * 512 + c0:(Q + 1) * 512], start=True, stop=not it["diag"])
                        if it["diag"]:
                            ins = e.matmul(bank(s_)[:, 0:n], lhsT=ident, rhs=sbmask[:, 0:n], start=False, stop=True)
                    return ins
                S.op("tensor", f, reads=qk_all + [("cst",)], excl=PB(0, 1))

            def A12(it):
                n, w = it["n"], it["w"]
                S.op("scalar", lambda e: e.activation(out=e_b[:, w, :, 0:n], in_=z2[:, :, 0:n], func=AF.Exp),
                     writes=[("e", w)], excl=PB(0, 1))
                S.op("scalar", lambda e: e.activation(out=sp_b[:, w, :, 0:n], in_=e_b[:, w, :, 0:n], func=AF.Ln, bias=1.0),
                     reads=[("e", w)], writes=[("sp", w)])

            def P2(it):
                n, w, c0, cb = it["n"], it["w"], it["c0"], it["cb"]
                if it["first"]:
                    S.op("vector", lambda e: e.memset(bank(cb, 2), 0.0), excl=PB(cb, cb + 1))

                def f(e):
                    ins = None
                    for s_ in range(2):
                        ins = e.matmul(bank(cb + s_)[:, c0:512], lhsT=tri, rhs=sp_b[:, w, s_, 0:n], start=False, stop=False, skip_group_check=True)
                    return ins
                S.op("tensor", f, reads=[("sp", w), ("cst",)], excl=PB(cb, cb + 1))

            def A3(it):
                n, c0, cb, g = it["n"], it["c0"], it["cb"], it["g"]
                S.op("scalar", lambda e: e.activation(out=g_b[:, g, :, 0:n], in_=C2(it)[:, :, c0:512], func=AF.Exp, scale=-1.0),
                     writes=[("G", g)], excl=PB(cb, cb + 1))

            def P3(it):
                n, w, c0, cb = it["n"], it["w"], it["c0"], it["cb"]
                if it["last"]:
                    return

                def f(e):
                    ins = None
                    for s_ in range(2):
                        ins = e.matmul(bank(cb + s_)[:, c0:512], lhsT=ctri, rhs=sp_b[:, w, s_, 0:n], start=False, stop=False, skip_group_check=True)
                    return ins
                S.op("tensor", f, reads=[("sp", w), ("cst",)], excl=PB(cb, cb + 1))

            def D1(it):
                n, w, g = it["n"], it["w"], it["g"]
                S.op("vector", lambda e: e.tensor_tensor(out=a_b[:, g, :, 0:n], in0=e_b[:, w, :, 0:n], in1=g_b[:, g, :, 0:n], op=ALU.mult),
                     reads=[("e", w), ("G", g)], writes=[("A", g)])

            oacc = [tmpA[:, 0:512], tmpB[:, 0:512]]

            def P4(it):
                n, c0, g, b = it["n"], it["c0"], it["g"], it["b"]

                def f(e):
                    ins = e.matmul(bank(6)[:, c0:512], lhsT=vv[:, 0, b, :], rhs=a_b[:, g, 0, 0:n], start=True, stop=False)
                    ins = e.matmul(bank(6)[:, c0:512], lhsT=vv[:, 1, b, :], rhs=a_b[:, g, 1, 0:n], start=False, stop=True)
                    return ins
                S.op("tensor", f, reads=[("A", g)] + vv_all, excl=PB(6))

            def ACC(it):
                c0, lane, Q = it["c0"], it["lane"], it["Q"]
                oa = oacc[lane]
                if it["first"]:
                    S.op("vector", lambda e: e.memset(oa, 0.0), writes=[("oacc", lane)])
                S.op("vector", lambda e: e.tensor_tensor(out=oa[:, c0:512], in0=bank(6)[:, c0:512], in1=oa[:, c0:512], op=ALU.add),
                     reads=[("oacc", lane)], writes=[("oacc", lane)], excl=PB(6))
                if it["last"]:
                    S.op("vector", lambda e: e.tensor_tensor(out=sq[:, Q * 512:(Q + 1) * 512], in0=oa, in1=oa, op=ALU.mult),
                         reads=[("oacc", lane)], writes=[("sq", Q)])
                    S.op("vector", lambda e: e.tensor_scalar(out=oT[:, pair, Q * 512:(Q + 1) * 512], in0=oa, scalar1=gout[:, pair:pair + 1], scalar2=None, op0=ALU.mult),
                         reads=[("oacc", lane), ("gout",)], writes=[("oT", pair, Q)])

            def at(j, fn):
                if 0 <= j < N:
                    fn(items[j])

            for i, it in enumerate(items):
                it["idx"] = i
            for i in range(-2, N + 3):
                at(i - 2, P4)
                at(i - 2, ACC)
                at(i - 1, A3)
                at(i - 1, P3)
                at(i - 1, D1)
                at(i, P2)
                at(i + 1, A12)
                at(i + 2, P1)
                if i >= 0 and extra:
                    extra.pop(0)()
            while extra:
                extra.pop(0)()

        def chunk_attention(pair, qT, kT, vv, oT, sq, E_b, E2_b, biasb, qk_all, vv_all, tail):
            items = []
            for m in range(NTB):
                for s in range(2):
                    items.append(dict(m=m, s=s, ro=64 * s, hh=2 * (pair - 4) + s, dmin=max(0, 4 - m)))
            n_it = len(items)
            for i, itm in enumerate(items):
                itm["z"] = i % 2
                itm["ob"] = 4 + (itm["m"] // 4) % 2
                itm["db"] = 6 + (itm["m"] // 4) % 2

            def zcols(it, lo, hi):
                base = 1024 * it["z"]
                return ps[:, base + lo: base + hi]

            def c1(it):
                def f(e, it=it):
                    ro, m, hh, dmin = it["ro"], it["m"], it["hh"], it["dmin"]
                    ins = None
                    for dl in range(dmin, 5):
                        kb = m - 4 + dl
                        ins = e.matmul(zcols(it, dl * 128, (dl + 1) * 128), lhsT=kT[:, kb * 128:(kb + 1) * 128],
                                       rhs=qT[:, it["s"], m * 128:(m + 1) * 128], start=True, stop=True)
                    return ins
                S.op("tensor", f, reads=qk_all, excl=PB(2 * it["z"], 2 * it["z"] + 1))

            def c2(it):
                lo, z = it["dmin"] * 128, it["z"]
                S.op("scalar", lambda e: e.activation(out=E_b[:, z, lo:640], in_=zcols(it, lo, 640), func=AF.Exp),
                     writes=[("E", z)], excl=PB(2 * z, 2 * z + 1))

            def c2b(it):
                lo, z, hh, z3 = it["dmin"] * 128, it["z"], it["hh"], it["idx"] % 3
                S.op("vector", lambda e: e.tensor_tensor(out=E2_b[:, z3, lo:640], in0=E_b[:, z, lo:640], in1=biasb[:, hh, lo:640], op=ALU.mult),
                     reads=[("E", z), ("biasb",)], writes=[("E2", z3)])

            def c3(it):
                def f(e, it=it):
                    ro, m, dmin, z = it["ro"], it["m"], it["dmin"], it["idx"] % 3
                    cg = (m % 4) * 128
                    ins = None
                    for dl in range(dmin, 5):
                        kb = m - 4 + dl
                        s_ = it["s"]
                        fl = dict(start=(dl == dmin), stop=(dl == 4)) if s_ == 0 else dict(start=False, stop=False, skip_group_check=True)
                        ins = e.matmul(bank(it["ob"])[:, cg:cg + 128], lhsT=vv[:, s_, kb, :], rhs=E2_b[:, z, dl * 128:(dl + 1) * 128], **fl)
                        ins = e.matmul(bank(it["db"])[:, cg:cg + 128], lhsT=ones_h[s_], rhs=E2_b[:, z, dl * 128:(dl + 1) * 128], **fl)
                    return ins
                S.op("tensor", f, reads=[("E2", it["idx"] % 3), ("cst",)] + vv_all, excl=PB(it["ob"], it["db"]))
                if it["m"] % 4 == 3 and it["s"] == 1:
                    Qc = it["m"] // 4
                    ob, db = it["ob"], it["db"]
                    hs = Qc % 2

                    def evac():
                        S.op("scalar", lambda e: e.activation(out=tmpA[:, hs * 512:(hs + 1) * 512], in_=bank(db), func=AF.Ln),
                             writes=[("tmpA", hs)], excl=PB(db))
                        S.op("scalar", lambda e: e.activation(out=tmpA[:, hs * 512:(hs + 1) * 512], in_=tmpA[:, hs * 512:(hs + 1) * 512], func=AF.Exp, scale=-1.0),
                             reads=[("tmpA", hs)], writes=[("tmpA", hs)])
                        S.op("vector", lambda e: e.tensor_tensor(out=tmpB[:, hs * 512:(hs + 1) * 512], in0=bank(ob), in1=tmpA[:, hs * 512:(hs + 1) * 512], op=ALU.mult),
                             reads=[("tmpA", hs)], writes=[("tmpB", hs)], excl=PB(ob))

                    def evac2():
                        S.op("vector", lambda e: e.tensor_tensor(out=sq[:, Qc * 512:(Qc + 1) * 512], in0=tmpB[:, hs * 512:(hs + 1) * 512],
                                                                 in1=tmpB[:, hs * 512:(hs + 1) * 512], op=ALU.mult),
                             reads=[("tmpB", hs)], writes=[("sq", Qc)])

                    def evac3():
                        S.op("vector", lambda e: e.tensor_scalar(out=oT[:, pair, Qc * 512:(Qc + 1) * 512], in0=tmpB[:, hs * 512:(hs + 1) * 512],
                                                                 scalar1=gout[:, pair:pair + 1], scalar2=None, op0=ALU.mult),
                             reads=[("tmpB", hs), ("gout",)], writes=[("oT", pair, Qc)])
                    deferred[it["idx"] + 4 + 2] = evac
                    deferred[it["idx"] + 4 + 3] = evac2
                    deferred[it["idx"] + 4 + 4] = evac3

            deferred = {}
            for i, itm in enumerate(items):
                itm["idx"] = i
            stages = [(0, c1), (1, c2), (2, c2b), (4, c3)]
            for t in range(n_it + 9):
                if t in deferred:
                    deferred.pop(t)()
                for d, fn in reversed(stages):
                    i = t - d
                    if 0 <= i < n_it:
                        fn(items[i])
                if tail:
                    nq = len(tail)
                    if n_it <= t < n_it + 4 and nq > 4:
                        tail.pop(0)()
                        tail.pop(0)()
                    elif t >= n_it + 4:
                        tail.pop(0)()
                        if tail:
                            tail.pop(0)()
            while tail:
                tail.pop(0)()
            assert not deferred

        def ple():
            S.barrier()
            A.reset(base_mark)
            wgate_ = A.alloc([8, 1024], BF16)
            wgate = xT
            wple = wgu[:, 0, :, :]
            hT = A.alloc([2, 8, 128], BF16)
            pb = A.alloc([2, 256], BF16)
            pT = A.alloc([4, 2, 128], BF16)
            ob = A.alloc([2, D], F32)
            tS = A.alloc([6, D], F32)
            tT = A.alloc([2, D], F32)
            jk = A.alloc([2, D], BF16)
            scol = {}

            def p0(tb):
                sl = tb % 2
                S.dma("gpsimd", [lambda e: e.dma_start(out=pb[:, sl, :], in_=p_d[tb * 128:(tb + 1) * 128, :])],
                      "p%d" % sl, writes=[("pb", sl)])
                S.op("scalar", lambda e: e.activation(out=xn_tm[:, sl, :], in_=h[:, tb, :], func=AF.Copy),
                     reads=[("h", tb)], writes=[("xn", sl)])

            def p1(tb):
                sl = tb % 2

                def tr(e):
                    ins = None
                    for kc in range(8):
                        ins = e.transpose(out=bank_bf(0)[:, kc * 128:(kc + 1) * 128], in_=xn_tm[:, sl, kc * 128:(kc + 1) * 128], identity=ident)
                    for c in range(2):
                        ins = e.transpose(out=bank_bf(1)[:, c * 128:(c + 1) * 128], in_=pb[:, sl, c * 128:(c + 1) * 128], identity=ident)
                    return ins
                S.op("tensor", tr, reads=[("xn", sl), ("pb", sl), ("cst",)], excl=PB(0, 1))

            def p2(tb):
                sl = tb % 2
                S.op("vector", lambda e: e.tensor_copy(out=hT[:, sl, :, :], in_=bank_bf(0).rearrange("p (k t) -> p k t", k=8)),
                     writes=[("hT", sl)], excl=PB(0))
                S.op("vector", lambda e: e.tensor_copy(out=pT[:, tb % 4, :, :], in_=bank_bf(1)[:, 0:256].rearrange("p (k t) -> p k t", k=2)),
                     writes=[("pT", tb % 4)], excl=PB(1))

            def p3(tb):
                sl = tb % 2
                gbk = 2 if sl == 0 else 4

                def mmg(e):
                    ins = None
                    for dh in range(2):
                        for kc in range(8):
                            ins = e.matmul(bank(gbk + dh), lhsT=hT[:, sl, kc, :], rhs=wgate[:, kc, dh * 512:(dh + 1) * 512], start=(kc == 0), stop=(kc == 7))
                    return ins
                S.op("tensor", mmg, reads=[("hT", sl)] + [("wgate", j) for j in range(8)], excl=PB(gbk, gbk + 1))

            def p4(tb):
                sl, s6 = tb % 2, tb % 6
                gbk = 2 if sl == 0 else 4
                S.op("scalar", lambda e: e.activation(out=tS[:, s6, :], in_=bank(gbk, 2), func=AF.Exp, scale=-1.0), writes=[("tS", s6)], excl=PB(gbk, gbk + 1))
                S.op("scalar", lambda e: e.activation(out=tS[:, s6, :], in_=tS[:, s6, :], func=AF.Ln, bias=1.0), reads=[("tS", s6)], writes=[("tS", s6)])
                S.op("scalar", lambda e: e.activation(out=tS[:, s6, :], in_=tS[:, s6, :], func=AF.Exp, scale=-1.0), reads=[("tS", s6)], writes=[("tS", s6)])

            def p5(tb):
                s4 = tb % 4

                def mmp(e):
                    ins = None
                    for dh in range(2):
                        for c in range(2):
                            ins = e.matmul(bank(6 + dh), lhsT=pT[:, s4, c, :], rhs=wple[:, c, dh * 512:(dh + 1) * 512], start=(c == 0), stop=(c == 1))
                    return ins
                S.op("tensor", mmp, reads=[("pT", s4), ("wple",)], excl=PB(6, 7))

            def p6(tb):
                s6 = tb % 6
                S.op("vector", lambda e: e.tensor_tensor(out=tS[:, s6, :], in0=bank(6, 2), in1=tS[:, s6, :], op=ALU.mult),
                     reads=[("tS", s6)], writes=[("tS", s6)], excl=PB(6, 7))

            def p7(tb):
                sl, s6 = tb % 2, tb % 6
                c = stat_cols(2)
                scol[tb] = c
                S.op("scalar", lambda e: e.activation(out=jk[:, sl, :], in_=tS[:, s6, :], func=AF.Square, accum_out=stat[:, c:c + 1]),
                     reads=[("tS", s6)], writes=[("jk", sl), ("stat", c)])
                rstd_from_ss(stat[:, c:c + 1], stat[:, c + 1:c + 2], 1, 1.0 / D, [("stat", c)], [("stat", c + 1)], True)

            def p8(tb):
                sl, s6 = tb % 2, tb % 6
                c = scol[tb]
                S.op("vector", lambda e: e.scalar_tensor_tensor(out=tT[:, sl, :], in0=tS[:, s6, :], scalar=stat[:, c + 1:c + 2], in1=gbuf[:, 0, :],
                                                                op0=ALU.mult, op1=ALU.mult),
                     reads=[("tS", s6), ("stat", c + 1), ("gbuf", 0)], writes=[("tT", sl)])
                S.op("vector", lambda e: e.tensor_tensor(out=ob[:, sl, :], in0=h[:, tb, :], in1=tT[:, sl, :], op=ALU.add),
                     reads=[("tT", sl), ("h", tb)], writes=[("ob", sl)])
                S.dma("sync", [lambda e: e.dma_start(out=out_d[tb * 128:(tb + 1) * 128, :], in_=ob[:, sl, :])],
                      "out%d" % sl, reads=[("ob", sl)], writes=[("out", tb)])

            def run(fn, i):
                if 0 <= i < NTB:
                    fn(i)
            for t in range(NTB + 5):
                run(p0, t)
                run(p1, t)
                run(p8, t - 5)
                run(p7, t - 4)
                run(p6, t - 3)
                run(p4, t - 2)
                run(p5, t - 2)
                run(p3, t - 1)
                run(p2, t)

        ffn(0, G_FFN1_PRE, G_FFN1_POST)
        attention_prefetch()
        attention()
        ffn_prefetch(1, G_FFN2_PRE, G_FFN2_POST)
        ffn(1, G_FFN2_PRE, G_FFN2_POST)
        ple()
        S.finish(nc, block, final_waits=["out0", "out1"])
    return nc


def _slab_cols(w):
    K, Fd = w.shape
    return np.ascontiguousarray(w.reshape(K // 128, 128, Fd // 128, 128).transpose(2, 1, 0, 3).reshape(Fd // 128, 128, (K // 128) * 128))


def _rows_part(w):
    R, C = w.shape
    return np.ascontiguousarray(w.reshape(R // 128, 128, C).transpose(1, 0, 2).reshape(128, (R // 128) * C))


def _consts():
    c = np.zeros((128, 1280), np.float32)
    j = np.arange(128)
    c[:, 0:128] = np.eye(128, dtype=np.float32)
    c[:, 128:256] = (j[:, None] >= j[None, :])
    c[:, 256:384] = (j[:, None] < j[None, :])
    c[:, 384:512] = 1.0
    y = np.arange(512)
    c[:, 512:576] = 1.0
    c[:, 704:768] = 1.0
    c[:, 768:1280] = np.where(j[:, None] < y[None, :], 0.0, NEG)
    return c


def _bias_gather(rel_bias):
    sk = np.arange(128)[:, None, None]
    dl = np.arange(5)[None, :, None]
    tq = np.arange(128)[None, None, :]
    idx = np.clip(128 * (4 - dl) + tq - sk, -128, 128) + 128
    g = rel_bias[:, idx]
    g = g.transpose(1, 0, 2, 3)
    compact = np.ascontiguousarray(g[:, :, 3:5, :].reshape(128, 8 * 256)).astype(np.float32)
    const = np.ascontiguousarray(np.broadcast_to(rel_bias[None, :, 256], (128, 8))).astype(np.float32)
    return compact, const


_NC_CACHE = {}


def _prep_shared(inp):
    f = lambda a: np.asarray(a, dtype=np.float32)
    sh = {}
    sh["g7"] = np.ascontiguousarray(np.stack([f(inp[k])[0] for k in (
        "g_ffn1_pre", "g_ffn1_post", "g_mix_pre", "g_mix_post", "g_ffn2_pre", "g_ffn2_post", "g_ple_post")], 0))
    go = np.concatenate([f(inp["g_out_sb"])[0], f(inp["g_out_ch"])[0]])
    sh["gout"] = np.ascontiguousarray(go.reshape(8, 128).T)
    sh["cst"] = _consts()
    sh["biasq"], sh["bconst"] = _bias_gather(f(inp["rel_bias"])[0])
    sh["wg1"] = _slab_cols(f(inp["w_ffn1_gate"])[0])
    sh["wu1"] = _slab_cols(f(inp["w_ffn1_up"])[0])
    sh["wd1"] = _rows_part(f(inp["w_ffn1_down"])[0])
    sh["wg2"] = _slab_cols(f(inp["w_ffn2_gate"])[0])
    sh["wu2"] = _slab_cols(f(inp["w_ffn2_up"])[0])
    sh["wd2"] = _rows_part(f(inp["w_ffn2_down"])[0])
    win = f(inp["w_in"])[0]
    slabs = _slab_cols(win)
    wl = np.zeros((8, 128, 3 * 1024), np.float32)
    for pair in range(8):
        g, pp = pair // 4, pair % 4
        for m in range(3):
            wl[pair, :, m * 1024:(m + 1) * 1024] = slabs[g * 12 + m * 4 + pp]
    sh["win"] = wl
    sh["wout"] = _rows_part(f(inp["w_out"])[0])
    sh["wgate"] = _rows_part(f(inp["w_ple_gate"])[0])
    sh["wple"] = _rows_part(f(inp["w_ple_proj"])[0])
    return sh


def kernel(**inputs):
    x = np.asarray(inputs["x"], dtype=np.float32)
    p = np.asarray(inputs["p"], dtype=np.float32)
    sh = _prep_shared(inputs)
    if "nc" not in _NC_CACHE:
        _NC_CACHE["nc"] = build()
    nc = _NC_CACHE["nc"]
    n = 8
    in_maps = []
    for c in range(n):
        m = dict(sh)
        m["x"] = np.ascontiguousarray(x[c])
        m["p"] = np.ascontiguousarray(p[0, c])
        in_maps.append(m)
    res = run_bass_kernel_spmd(nc, in_maps, core_ids=list(range(n)))
    return np.stack([np.asarray(r["out"], dtype=np.float32) for r in res.results], 0)
```

```python
import numpy as np
import concourse.bass as bass
import concourse.mybir as mybir
from concourse.bass_utils import run_bass_kernel_spmd

F32 = mybir.dt.float32
BF16 = mybir.dt.bfloat16
AF = mybir.ActivationFunctionType
ALU = mybir.AluOpType

ENGS = ("sync", "scalar", "vector", "gpsimd", "tensor")


class Sched:
    def __init__(self):
        self.prog = {e: [] for e in ENGS}
        self.cnt = {}
        self.known = {e: {} for e in ENGS}
        self.last_w = {}
        self.readers = {}
        self.last_x = {}

    def _emit(self, eng, fn, reads, writes, excl, sem, inc):
        toks = []
        for r in reads:
            t = self.last_w.get(r)
            if t is not None:
                toks.append(t)
        for w in writes:
            t = self.last_w.get(w)
            if t is not None:
                toks.append(t)
            toks.extend(self.readers.get(w, ()))
        for x in excl:
            t = self.last_x.get(x)
            if t is not None and not (t[2] == eng and eng == "tensor"):
                toks.append((t[0], t[1], t[2]))
        kn = self.known[eng]
        need = {}
        for (s, v, src) in toks:
            if src == eng and eng == "tensor" and s == "tensor":
                continue
            if kn.get(s, 0) < v and need.get(s, 0) < v:
                need[s] = v
        for s, v in need.items():
            kn[s] = v
        self.cnt[sem] = self.cnt.get(sem, 0) + inc
        tok = (sem, self.cnt[sem], eng if sem == eng else "dma:" + sem)
        self.prog[eng].append((tuple(need.items()), fn, sem, inc))
        for r in reads:
            self.readers.setdefault(r, []).append(tok)
        for w in writes:
            self.last_w[w] = tok
            self.readers[w] = []
        for x in excl:
            self.last_x[x] = (tok[0], tok[1], eng)
        return tok

    def op(self, eng, fn, reads=(), writes=(), excl=()):
        return self._emit(eng, fn, reads, writes, excl, eng, 1)

    def dma(self, eng, fns, sem, reads=(), writes=()):
        return self._emit(eng, list(fns), reads, writes, (), sem, 16 * len(fns))

    def finish(self, nc, block, final_waits=()):
        names = set(ENGS) | {s for e in ENGS for (_, _, s, _) in self.prog[e] if s is not None}
        sems = {}
        import contextlib
        self._stack = contextlib.ExitStack()
        for n in sorted(names):
            sems[n] = self._stack.enter_context(nc.semaphore("s_" + n))
        finals = [(s, self.cnt[s]) for s in final_waits]
        allfin = [(e, self.cnt.get(e, 0)) for e in ENGS if self.cnt.get(e, 0) > 0]

        def make(engname):
            def f(e):
                for (waits, fn, s, inc) in self.prog[engname]:
                    for (ws, wv) in waits:
                        e.wait_ge(sems[ws], wv)
                    if fn is None:
                        continue
                    if isinstance(fn, list):
                        for f1 in fn:
                            f1(e).then_inc(sems[s], 16)
                    else:
                        fn(e).then_inc(sems[s], inc)
                if engname == "sync":
                    for (s, v) in finals + allfin:
                        e.wait_ge(sems[s], v)
            return f

        block.sync(make("sync"))
        block.scalar(make("scalar"))
        block.vector(make("vector"))
        block.gpsimd(make("gpsimd"))
        block.tensor(make("tensor"))

    def barrier(self):
        for eng in ENGS:
            kn = self.known[eng]
            need = {s: v for s, v in self.cnt.items() if kn.get(s, 0) < v}
            for s, v in need.items():
                kn[s] = v
            if need:
                self.prog[eng].append((tuple(need.items()), None, None, 0))


T = 2048
D = 1024
NTB = 16
FF = 2816
NFC = 22
EPS = 1e-6
NEG = -30000.0

G_FFN1_PRE, G_FFN1_POST, G_MIX_PRE, G_MIX_POST, G_FFN2_PRE, G_FFN2_POST, G_PLE = range(7)


class Arena:
    def __init__(self, ap, nbytes):
        self.ap = ap
        self.nbytes = nbytes
        self.off = 0

    def mark(self):
        return self.off

    def reset(self, m):
        self.off = m

    def alloc(self, shape, dt):
        n = int(np.prod(shape))
        nb = n * (4 if dt == F32 else 2)
        nb_al = (nb + 31) // 32 * 32
        assert self.off + nb_al <= self.nbytes, (self.off, nb_al, self.nbytes)
        a = self.ap[:, self.off // 2: (self.off + nb) // 2]
        self.off += nb_al
        if dt == F32:
            a = a.bitcast(F32)
        if len(shape) == 2:
            a = a.rearrange("p (a b) -> p a b", a=shape[0])
        elif len(shape) == 3:
            a = a.rearrange("p (a b c) -> p a b c", a=shape[0], b=shape[1])
        return a


ARENA_BYTES = 212800


def build(debug=False):
    nc = bass.Bass("TRN2", target_bir_lowering=False)
    dt_in = lambda n, s: nc.dram_tensor(n, s, F32, kind="ExternalInput").ap()
    x_d = dt_in("x", [T, D])
    p_d = dt_in("p", [T, 256])
    g7_d = dt_in("g7", [7, D])
    gout_d = dt_in("gout", [128, 8])
    cst_d = dt_in("cst", [128, 6 * 128 + 512])
    biasq_d = dt_in("biasq", [128, 8 * 256])
    bconst_d = dt_in("bconst", [128, 8])
    wg_d = [dt_in("wg1", [NFC, 128, 1024]), dt_in("wg2", [NFC, 128, 1024])]
    wu_d = [dt_in("wu1", [NFC, 128, 1024]), dt_in("wu2", [NFC, 128, 1024])]
    wd_d = [dt_in("wd1", [128, NFC * 1024]), dt_in("wd2", [128, NFC * 1024])]
    win_d = dt_in("win", [8, 128, 3 * 1024])
    wout_d = dt_in("wout", [128, 8 * 1024])
    wgate_d = dt_in("wgate", [128, 8 * 1024])
    wple_d = dt_in("wple", [128, 2 * 1024])
    out_d = nc.dram_tensor("out", [T, D], F32, kind="ExternalOutput").ap()

    S = Sched()
    with (
        nc.sbuf_tensor("arena", [128, ARENA_BYTES // 2], BF16) as arena_t,
        nc.psum_tensor("ps", [128, 4096], F32) as ps,
        nc.Block() as block,
    ):
        A = Arena(arena_t, ARENA_BYTES)
        h = A.alloc([NTB, D], F32)
        cst = A.alloc([6, 128], BF16)
        ident, tri, ctri, ones = cst[:, 0, :], cst[:, 1, :], cst[:, 2, :], cst[:, 3, :]
        ones_h = [cst[:, 4, :], cst[:, 5, :]]
        sbmask = A.alloc([512], BF16)
        gbuf = A.alloc([2, D], F32)
        gout = A.alloc([8], F32)
        bconst = A.alloc([8], F32)
        stat = A.alloc([256], F32)
        junk = A.alloc([D], BF16)
        tmpA = A.alloc([D], F32)
        tmpB = A.alloc([D], F32)
        xn_tm = A.alloc([2, D], BF16)
        base_mark = A.mark()

        def bank(b, n=1):
            return ps[:, b * 512:(b + n) * 512]

        def bank_bf(b):
            return ps[:, b * 512:(b + 1) * 512].bitcast(BF16)

        PB = lambda *bs: [("ps", b) for b in bs]
        TP = lambda ro: ({"tile_position": (0, ro)} if ro else {})

        stat_ctr = [0]

        def stat_cols(n):
            c = stat_ctr[0]
            if c + n > 112:
                c = 0
            stat_ctr[0] = c + n
            return c

        S.dma("gpsimd", [lambda e: e.dma_start(out=cst.rearrange("p a b -> p (a b)"), in_=cst_d[:, 0:768]),
                         lambda e: e.dma_start(out=sbmask, in_=cst_d[:, 768:1280])],
              "cst", writes=[("cst",)])
        S.dma("sync", [lambda e: e.dma_start(out=gout, in_=gout_d), lambda e: e.dma_start(out=bconst, in_=bconst_d)], "gout", writes=[("gout",), ("bconst",)])

        def load_gains(i0, i1, half1):
            S.dma("sync", [lambda e: e.dma_start(out=gbuf[:, 0, :], in_=g7_d[i0:i0 + 1, :].broadcast_to([128, D]))],
                  "g0", writes=[("gbuf", 0)])
            if i1 is not None:
                S.dma("sync", [lambda e: e.dma_start(out=gbuf[:, 1, :], in_=g7_d[i1:i1 + 1, :].broadcast_to([128, D]))],
                      "g1", writes=[("gbuf", 1)])
                if half1:
                    S.op("vector", lambda e: e.tensor_scalar(out=gbuf[:, 1, :], in0=gbuf[:, 1, :], scalar1=0.5, scalar2=None, op0=ALU.mult),
                         reads=[("gbuf", 1)], writes=[("gbuf", 1)])

        def rstd_from_ss(ss_ap, r_ap, n, inv_n, rkeys_in, wkeys_out, use_lnexp=False):
            if not use_lnexp:
                S.op("scalar", lambda e: e.activation(out=r_ap, in_=ss_ap, func=AF.Sqrt, scale=inv_n, bias=EPS),
                     reads=rkeys_in, writes=wkeys_out)
                S.op("vector", lambda e: e.reciprocal(out=r_ap, in_=r_ap), reads=wkeys_out, writes=wkeys_out)
            else:
                S.op("vector", lambda e: e.tensor_scalar(out=r_ap, in0=ss_ap, scalar1=inv_n, scalar2=EPS, op0=ALU.mult, op1=ALU.add),
                     reads=rkeys_in, writes=wkeys_out)
                S.op("scalar", lambda e: e.activation(out=r_ap, in_=r_ap, func=AF.Ln), reads=wkeys_out, writes=wkeys_out)
                S.op("scalar", lambda e: e.activation(out=r_ap, in_=r_ap, func=AF.Exp, scale=-0.5), reads=wkeys_out, writes=wkeys_out)

        TRB = 6
        xn_ctr = [0]

        def sq_stat(tb, col):
            S.op("scalar", lambda e: e.activation(out=junk, in_=h[:, tb, :], func=AF.Square, accum_out=stat[:, col:col + 1]),
                 reads=[("h", tb)], writes=[("junk",), ("stat", col)])

        def norm_stats(tbs):
            n = len(tbs)
            c0 = stat_cols(2 * n)
            for i, tb in enumerate(tbs):
                S.op("scalar", lambda e, tb=tb, i=i: e.activation(out=junk, in_=h[:, tb, :], func=AF.Square, accum_out=stat[:, c0 + i:c0 + i + 1]),
                     reads=[("h", tb)], writes=[("junk",), ("stat", c0 + i)])
            rstd_from_ss(stat[:, c0:c0 + n], stat[:, c0 + n:c0 + 2 * n], n, 1.0 / D,
                         [("stat", c0 + i) for i in range(n)], [("stat", c0 + n + i) for i in range(n)])
            return c0

        def nx_a(tb, i, c0, n):
            sl = xn_ctr[0] % 2
            xn_ctr[0] += 1
            S.op("vector", lambda e: e.scalar_tensor_tensor(
                out=xn_tm[:, sl, :], in0=h[:, tb, :], scalar=stat[:, c0 + n + i:c0 + n + i + 1], in1=gbuf[:, 0, :],
                op0=ALU.mult, op1=ALU.mult),
                reads=[("h", tb), ("stat", c0 + n + i), ("gbuf", 0)], writes=[("xn", sl)])
            return sl

        def norm_xpose_one(tb, i, c0, n, dstT, dst_key, col, trb, evac_eng):
            sl = nx_a(tb, i, c0, n)
            nx_b(sl, dstT, dst_key, col, trb, evac_eng)

        def nx_b(sl, dstT, dst_key, col, trb, evac_eng):
            tps = bank_bf(trb)

            def tr(e):
                ins = None
                for kc in range(8):
                    ins = e.transpose(out=tps[:, kc * 128:(kc + 1) * 128], in_=xn_tm[:, sl, kc * 128:(kc + 1) * 128], identity=ident)
                return ins
            S.op("tensor", tr, reads=[("xn", sl), ("cst",)], excl=PB(trb))
            if evac_eng == "scalar":
                S.op("scalar", lambda e: e.activation(out=dstT[:, :, col:col + 128], in_=tps.rearrange("p (k t) -> p k t", k=8), func=AF.Copy),
                     writes=[(dst_key, col // 128)], excl=PB(trb))
            else:
                S.op("vector", lambda e: e.tensor_copy(out=dstT[:, :, col:col + 128], in_=tps.rearrange("p (k t) -> p k t", k=8)),
                     writes=[(dst_key, col // 128)], excl=PB(trb))

        def norm_transpose(tbs, dstT, dst_key, col_of, trbs=(6, 7), batched=True, c0=None):
            n = len(tbs)
            if batched:
                if c0 is None:
                    c0 = norm_stats(tbs)
                for i, tb in enumerate(tbs):
                    norm_xpose_one(tb, i, c0, n, dstT, dst_key, col_of(tb), trbs[i % len(trbs)], "scalar" if i % 2 == 0 else "vector")
            else:
                for i, tb in enumerate(tbs):
                    c1 = norm_stats([tb])
                    norm_xpose_one(tb, 0, c1, 1, dstT, dst_key, col_of(tb), trbs[i % len(trbs)], "scalar" if i % 2 == 0 else "vector")

        def post_norm_residual(src_ap, src_reads, src_excl, tb, gslot, use_lnexp=False, out_ap=None, out_key=None):
            c = stat_cols(2)
            S.op("scalar", lambda e: e.activation(out=junk, in_=src_ap, func=AF.Square, accum_out=stat[:, c:c + 1]),
                 reads=src_reads, writes=[("junk",), ("stat", c)], excl=src_excl)
            rstd_from_ss(stat[:, c:c + 1], stat[:, c + 1:c + 2], 1, 1.0 / D, [("stat", c)], [("stat", c + 1)], use_lnexp)
            S.op("vector", lambda e: e.scalar_tensor_tensor(out=tmpA, in0=src_ap, scalar=stat[:, c + 1:c + 2], in1=gbuf[:, gslot, :],
                                                            op0=ALU.mult, op1=ALU.mult),
                 reads=list(src_reads) + [("stat", c + 1), ("gbuf", gslot)], writes=[("tmpA",)], excl=src_excl)
            if out_ap is None:
                S.op("vector", lambda e: e.tensor_tensor(out=h[:, tb, :], in0=h[:, tb, :], in1=tmpA, op=ALU.add),
                     reads=[("tmpA",), ("h", tb)], writes=[("h", tb)])
            else:
                S.op("vector", lambda e: e.tensor_tensor(out=out_ap, in0=h[:, tb, :], in1=tmpA, op=ALU.add),
                     reads=[("tmpA",), ("h", tb)], writes=[out_key])

        wgu_ctr = [0]

        A.reset(base_mark)
        xT = A.alloc([8, 1024], BF16)
        hid = A.alloc([NFC, 1024], BF16)
        wd = A.alloc([NFC, 1024], BF16)
        wgu = A.alloc([2, 2, 1024], BF16)
        sg = A.alloc([2, 512], BF16)
        A.reset(base_mark)
        pre = {}

        def load_wgu(li, fc):
            sl = wgu_ctr[0] % 2
            wgu_ctr[0] += 1
            S.dma("gpsimd", [lambda e: e.dma_start(out=wgu[:, sl, 0, :], in_=wg_d[li][fc]),
                             lambda e: e.dma_start(out=wgu[:, sl, 1, :], in_=wu_d[li][fc])],
                  "wgu%d" % sl, writes=[("wgu", sl)])
            return sl

        def ffn_prefetch(li, gpre, gpost):
            load_gains(gpre, gpost, True)
            pre["gains%d" % li] = True
            rstd_from_ss(stat[:, 112:120], stat[:, 120:128], 8, 1.0 / D,
                         [("stat", 112 + i) for i in range(8)], [("stat", 120 + i) for i in range(8)])
            pre["stats%d" % li] = 112
            pre["wgu%d" % li] = [load_wgu(li, 0), load_wgu(li, 1)]

        def ple_prefetch():
            for j in range(0, 8, 2):
                S.dma("gpsimd", [lambda e, j=j: e.dma_start(out=xT[:, j:j + 2, :], in_=wgate_d[:, j * 1024:(j + 2) * 1024].rearrange("p (a b) -> p a b", a=2))],
                      "wgate%d" % (j // 2), writes=[("wgate", j), ("wgate", j + 1)] + [("xT", i) for i in range(8)])
            S.dma("gpsimd", [lambda e: e.dma_start(out=wgu[:, 0, :, :], in_=wple_d.rearrange("p (a b) -> p a b", a=2))], "wple",
                  writes=[("wple",), ("wgu", 0)])
            load_gains(G_PLE, None, False)

        def ffn(li, gpre, gpost):
            S.barrier()
            def load_x(tb):
                S.dma("sync", [lambda e: e.dma_start(out=h[:, tb, :], in_=x_d[tb * 128:(tb + 1) * 128, :])],
                      "x%d" % tb, writes=[("h", tb)])
            if li == 0:
                for tb in range(8):
                    load_x(tb)
            if not pre.get("gains%d" % li):
                load_gains(gpre, gpost, True)
            if li == 0:
                for tb in range(8, NTB):
                    load_x(tb)

            def load_wd(j):
                S.dma("gpsimd", [lambda e: e.dma_start(out=wd[:, j:j + 2, :].rearrange("p a b -> p (a b)"),
                                                      in_=wd_d[li][:, j * 1024:(j + 2) * 1024])],
                      "wd%d" % (j // 2), writes=[("wd", j), ("wd", j + 1)])
            it = 0
            for hf in range(2):
                tbs = list(range(8 * hf, 8 * hf + 8))
                if hf == 0:
                    if "stats%d" % li in pre:
                        c0p = pre["stats%d" % li]
                        for i, tb in enumerate(tbs):
                            norm_xpose_one(tb, i, c0p, 8, xT, "xT", (tb % 8) * 128, 6 + i % 2, "scalar" if i % 2 == 0 else "vector")
                    elif li == 0:
                        cs = stat_cols(16)
                        sls = {}
                        for i in range(10):
                            if i < 8:
                                sq_stat(i, cs + i)
                            if 1 <= i <= 8:
                                j = i - 1
                                rstd_from_ss(stat[:, cs + j:cs + j + 1], stat[:, cs + 8 + j:cs + 9 + j], 1, 1.0 / D, [("stat", cs + j)], [("stat", cs + 8 + j)])
                                sls[j] = nx_a(j, j, cs, 8)
                            if i >= 2:
                                j = i - 2
                                nx_b(sls[j], xT, "xT", j * 128, 6 + j % 2, "vector" if j < 6 else "scalar")
                    else:
                        norm_transpose(tbs, xT, "xT", lambda tb: (tb % 8) * 128)
                for fc in range(NFC):
                    if hf == 0 and 2 <= fc < 13:
                        load_wd(2 * (fc - 2))
                    if hf == 0 and fc < 2 and pre.get("wgu%d" % li):
                        sl = pre["wgu%d" % li][fc]
                    else:
                        sl = load_wgu(li, fc)
                    for tt in range(2):
                        gb, ub = it % 2, 2 + it % 2
                        sgs = it % 2
                        it += 1

                        def mm(e, sl=sl, tt=tt, gb=gb, ub=ub):
                            ins = None
                            for m, bk in ((0, gb), (1, ub)):
                                for kc in range(8):
                                    ins = e.matmul(bank(bk), lhsT=wgu[:, sl, m, kc * 128:(kc + 1) * 128],
                                                   rhs=xT[:, kc, tt * 512:(tt + 1) * 512], start=(kc == 0), stop=(kc == 7))
                            return ins
                        S.op("tensor", mm, reads=[("wgu", sl)] + [("xT", tt * 4 + q) for q in range(4)], excl=PB(gb, ub))
                        S.op("scalar", lambda e, gb=gb, sgs=sgs: e.activation(out=sg[:, sgs, :], in_=bank(gb), func=AF.Silu),
                             writes=[("sg", sgs)], excl=PB(gb))
                        S.op("vector", lambda e, ub=ub, sgs=sgs, fc=fc, tt=tt: e.tensor_tensor(
                            out=hid[:, fc, tt * 512:(tt + 1) * 512], in0=bank(ub), in1=sg[:, sgs, :], op=ALU.mult),
                            reads=[("sg", sgs)], writes=[("hid", fc, tt)], excl=PB(ub))
                if hf == 0:
                    nx_c0 = norm_stats(list(range(8, 16)))
                if hf == 1 and li == 1:
                    ple_prefetch()
                if hf == 1 and li == 0:
                    S.dma("sync", [lambda e: e.dma_start(out=gbuf[:, 0, :], in_=g7_d[G_MIX_PRE:G_MIX_PRE + 1, :].broadcast_to([128, D]))],
                          "g0", writes=[("gbuf", 0)])
                    for tb_ in range(8):
                        sq_stat(tb_, 128 + tb_)
                    rstd_from_ss(stat[:, 128:136], stat[:, 144:152], 8, 1.0 / D,
                                 [("stat", 128 + i) for i in range(8)], [("stat", 144 + i) for i in range(8)])
                for tbl in range(8):
                    tb = 8 * hf + tbl
                    fb = 4 if tbl % 2 == 0 else 6

                    def mmd(e, tbl=tbl, fb=fb):
                        ins = None
                        for dh in range(2):
                            for fc in range(NFC):
                                ins = e.matmul(bank(fb + dh), lhsT=hid[:, fc, tbl * 128:(tbl + 1) * 128],
                                               rhs=wd[:, fc, dh * 512:(dh + 1) * 512], start=(fc == 0), stop=(fc == NFC - 1))
                        return ins
                    if hf == 0:
                        nsl = nx_a(8 + tbl, tbl, nx_c0, 8)
                    elif li == 0:
                        nsl = nx_a(tbl, tbl, 128, 16)
                    S.op("tensor", mmd, reads=[("hid", fc, tbl // 4) for fc in range(NFC)] + [("wd", fc) for fc in range(NFC)],
                         excl=PB(fb, fb + 1))
                    if hf == 0 or li == 0:
                        nx_b(nsl, xT, "xT", tbl * 128, tbl % 2, "scalar" if tbl % 2 == 0 else "vector")
                    post_norm_residual(bank(fb, 2), [], PB(fb, fb + 1), tb, 1)
                    if li == 0 and hf == 1:
                        sq_stat(tb, 128 + tb)

        def attention_prefetch():
            S.dma("sync", [lambda e: e.dma_start(out=gbuf[:, 1, :], in_=g7_d[G_MIX_POST:G_MIX_POST + 1, :].broadcast_to([128, D]))],
                  "g1", writes=[("gbuf", 1)])
            rstd_from_ss(stat[:, 136:144], stat[:, 152:160], 8, 1.0 / D,
                         [("stat", 136 + i) for i in range(8)], [("stat", 152 + i) for i in range(8)])
            pre["attn_stats"] = 128

        def attention():
            S.barrier()
            A.reset(base_mark)
            uTh = [A.alloc([8, 1024], BF16), A.alloc([8, 1024], BF16)]

            def uTs(kc, c0_, c1_):
                hf_ = c0_ // 1024
                return uTh[hf_][:, kc, c0_ - 1024 * hf_:c1_ - 1024 * hf_]
            oT = A.alloc([8, T], BF16)
            ep_mark = A.mark()
            qT = A.alloc([2, T], BF16)
            kT = A.alloc([T], BF16)
            vv = A.alloc([2, NTB, 128], BF16)
            wqkv = A.alloc([3, 1024], BF16)
            sq = A.alloc([T], BF16)
            NW = 3
            wk_mark = A.mark()
            e_b = A.alloc([NW, 2, 512], BF16)
            sp_b = A.alloc([NW, 2, 512], BF16)
            g_b = A.alloc([3, 2, 512], BF16)
            a_b = A.alloc([3, 2, 512], BF16)
            A.reset(wk_mark)
            E_b = A.alloc([2, 640], BF16)
            E2_b = A.alloc([3, 640], BF16)
            biasb = A.alloc([8, 640], BF16)
            ssacc = stat[:, 192:224]
            S.op("vector", lambda e: e.memset(ssacc, 0.0), writes=[("ssacc",)])
            S.op("vector", lambda e: e.memset(qT[64:128, 0, :], 0.0), writes=[("qT", i) for i in range(4)])
            S.op("vector", lambda e: e.memset(qT[0:64, 1, :], 0.0), writes=[("qT", i) for i in range(4)])
            S.op("vector", lambda e: e.memset(vv[:, 0, :, 64:128], 0.0), writes=[("vv", i) for i in range(4)])
            S.op("vector", lambda e: e.memset(vv[:, 1, :, 0:64], 0.0), writes=[("vv", i) for i in range(4)])
            for i_, tb_ in enumerate(range(8, NTB)):
                norm_xpose_one(tb_, tb_, 128, 16, uTh[1], "uT", (tb_ % 8) * 128, 6 + i_ % 2, "scalar" if i_ % 2 == 0 else "vector")
            uT_all = [("uT", i) for i in range(NTB)]
            wout = uTh[0]

            qT1 = oT[:, 4:6, :]
            vv1 = oT[:, 6:8, :].rearrange("p s (t c) -> p s t c", c=128)
            kT1 = xn_tm.rearrange("p a b -> p (a b)")
            gb0 = gbuf[:, 0, :].bitcast(BF16)
            sets = [
                dict(q=qT, k=kT, v=vv, w=[wqkv[:, 0, :], wqkv[:, 1, :], wqkv[:, 2, :]], kq="qT", kk="kT", kv="vv",
                     kw=[("wqkv",)], sem="win"),
                dict(q=qT1, k=kT1, v=vv1, w=[junk, gb0[:, 0:1024], gb0[:, 1024:2048]], kq="qT1", kk="kT1", kv="vv1",
                     kw=[("junk",), ("gbuf", 0), ("xn", 0), ("xn", 1)], sem="win1"),
            ]
            st1 = sets[1]
            S.op("vector", lambda e: e.memset(st1["q"][64:128, 0, :], 0.0), writes=[("qT1", i) for i in range(4)])
            S.op("vector", lambda e: e.memset(st1["q"][0:64, 1, :], 0.0), writes=[("qT1", i) for i in range(4)])
            S.op("vector", lambda e: e.memset(st1["v"][:, 0, :, 64:128], 0.0), writes=[("vv1", i) for i in range(4)])
            S.op("vector", lambda e: e.memset(st1["v"][:, 1, :, 0:64], 0.0), writes=[("vv1", i) for i in range(4)])

            def issue_w(pair, st):
                S.dma("gpsimd", [lambda e, m=m: e.dma_start(out=st["w"][m], in_=win_d[pair][:, m * 1024:(m + 1) * 1024]) for m in range(3)],
                      st["sem"], writes=st["kw"])

            def proj_closures(pair, st, overlapped):
                out = []
                units = [(m, t) for m in (0, 1) for t in range(4)] + [(2, t) for t in range(4)]
                for ui, (m, t) in enumerate(units):
                    if overlapped:
                        bk = 7
                    else:
                        bk = (t % 2) if m < 2 else 2 + t % 2
                    if m < 2:
                        mm = [(None, kc) for kc in range(8)]
                        splits = [3, 3, 2]
                        rkeys = st["kw"] + uT_all[t * 4:t * 4 + 4]
                    else:
                        mm = [(j, kc) for j in range(4) for kc in range(8)]
                        splits = [11, 11, 10]
                        rkeys = st["kw"] + uT_all[t * 4:t * 4 + 4]
                    if not overlapped:
                        splits = [len(mm)]
                    pos = 0
                    for si, n_ in enumerate(splits):
                        part = mm[pos:pos + n_]
                        pos += n_
                        last = (si == len(splits) - 1)

                        def emit(part=part, last=last, m=m, t=t, bk=bk, rkeys=rkeys):
                            def f(e):
                                ins = None
                                for (j, kc) in part:
                                    if m < 2:
                                        ins = e.matmul(bank(bk), lhsT=st["w"][m][:, kc * 128:(kc + 1) * 128], rhs=uTs(kc, t * 512, (t + 1) * 512),
                                                       start=(kc == 0), stop=(kc == 7))
                                    else:
                                        tb = 4 * t + j
                                        ins = e.matmul(bank(bk)[:, j * 128:(j + 1) * 128], lhsT=uTs(kc, tb * 128, (tb + 1) * 128),
                                                       rhs=st["w"][2][:, kc * 128:(kc + 1) * 128], start=(kc == 0), stop=(kc == 7))
                                return ins
                            S.op("tensor", f, reads=rkeys, excl=PB(bk))
                            if not last:
                                return
                            if m == 0:
                                for hh_, (lo, hi) in enumerate(((0, 64), (64, 128))):
                                    if overlapped:
                                        S.op("vector", lambda e, hh_=hh_, lo=lo, hi=hi: e.tensor_scalar(
                                            out=st["q"][lo:hi, hh_, t * 512:(t + 1) * 512], in0=bank(bk)[lo:hi, :], scalar1=0.125, scalar2=None, op0=ALU.mult),
                                            writes=[(st["kq"], t)], excl=PB(bk))
                                    else:
                                        S.op("scalar", lambda e, hh_=hh_, lo=lo, hi=hi: e.activation(
                                            out=st["q"][lo:hi, hh_, t * 512:(t + 1) * 512], in_=bank(bk)[lo:hi, :], func=AF.Copy, scale=0.125),
                                            writes=[(st["kq"], t)], excl=PB(bk))
                            elif m == 1:
                                S.op("vector", lambda e: e.tensor_copy(out=st["k"][:, t * 512:(t + 1) * 512], in_=bank(bk)),
                                     writes=[(st["kk"], t)], excl=PB(bk))
                            else:
                                src = bank(bk).rearrange("p (a b) -> p a b", a=4)
                                if overlapped:
                                    S.op("vector", lambda e: e.tensor_copy(out=st["v"][:, 0, 4 * t:4 * t + 4, 0:64], in_=src[:, :, 0:64]),
                                         writes=[(st["kv"], t)], excl=PB(bk))
                                else:
                                    S.op("scalar", lambda e: e.activation(out=st["v"][:, 0, 4 * t:4 * t + 4, 0:64], in_=src[:, :, 0:64], func=AF.Copy),
                                         writes=[(st["kv"], t)], excl=PB(bk))
                                S.op("vector", lambda e: e.tensor_copy(out=st["v"][:, 1, 4 * t:4 * t + 4, 64:128], in_=src[:, :, 64:128]),
                                     writes=[(st["kv"], t)], excl=PB(bk))
                        out.append(emit)
                return out

            def ss_pair(grp, bk=0):
                def mms(e):
                    ins = None
                    for tb in range(NTB):
                        ins = e.matmul(bank(bk)[:, tb:tb + 1], lhsT=sq[:, tb * 128:(tb + 1) * 128], rhs=ones[:, 0:1], start=True, stop=True)
                    return ins
                S.op("tensor", mms, reads=[("sq", i) for i in range(4)] + [("cst",)], excl=PB(bk))
                S.op("vector", lambda e: e.tensor_tensor(out=ssacc[:, grp * 16:(grp + 1) * 16], in0=bank(bk)[:, 0:16],
                                                         in1=ssacc[:, grp * 16:(grp + 1) * 16], op=ALU.add),
                     reads=[("ssacc",)], writes=[("ssacc",)], excl=PB(bk))

            def keys_of(st):
                return ([(st["kq"], i) for i in range(4)] + [(st["kk"], i) for i in range(4)], [(st["kv"], i) for i in range(4)])

            issue_w(0, sets[0])
            for c in proj_closures(0, sets[0], False):
                c()
            plist = []
            for pair in range(4):
                st = sets[pair % 2]
                nst = sets[(pair + 1) % 2]
                qk_all, vv_all = keys_of(st)
                plist.append(dict(pair=pair, q=st["q"], k=st["k"], v=st["v"], qk=qk_all, vvk=vv_all,
                                  start=(lambda pair=pair, nst=nst: issue_w(pair + 1, nst)),
                                  extra=(lambda pair=pair, nst=nst: proj_closures(pair + 1, nst, True)),
                                  finish=(lambda: ss_pair(0, 6))))
            sb_attention(plist, oT, sq, e_b, sp_b, g_b, a_b, NW)
            for pair in range(4, 8):
                st = sets[0]
                if pair == 4:
                    S.barrier()
                    S.dma("gpsimd", [lambda e: e.dma_start(out=biasb[:, :, 384:640], in_=biasq_d.rearrange("p (a b) -> p a b", a=8))], "biasq", writes=[("biasb",)])
                    for hh in range(8):
                        S.op("vector", lambda e, hh=hh: e.tensor_scalar(out=biasb[:, hh, 0:384], in0=sbmask[:, 0:384], scalar1=0.0, scalar2=bconst[:, hh:hh + 1],
                                                                       op0=ALU.mult, op1=ALU.add),
                             reads=[("bconst",), ("cst",), ("biasb",)], writes=[("biasb",)])
                    for hh in range(8):
                        S.op("vector", lambda e, hh=hh: e.memset(biasb[64:128, hh, 512:576], NEG), reads=[("biasb",)], writes=[("biasb",)])
                        S.op("vector", lambda e, hh=hh: e.memset(biasb[0:64, hh, 64:128], NEG), reads=[("biasb",)], writes=[("biasb",)])
                    S.op("scalar", lambda e: e.activation(out=biasb.rearrange("p a b -> p (a b)"), in_=biasb.rearrange("p a b -> p (a b)"), func=AF.Exp),
                         reads=[("biasb",)], writes=[("biasb",)])
                if pair == 7:
                    for j in range(0, 8, 2):
                        S.dma("gpsimd", [lambda e, j=j: e.dma_start(out=wout[:, j:j + 2, :],
                                                                   in_=wout_d[:, j * 1024:(j + 2) * 1024].rearrange("p (a b) -> p a b", a=2))],
                              "wout%d" % (j // 2), writes=[("wout", j), ("wout", j + 1)] + uT_all)
                qk_all, vv_all = keys_of(st)
                tail = []
                if pair < 7:
                    issue_w(pair + 1, st)
                    tail = proj_closures(pair + 1, st, False)
                chunk_attention(pair, st["q"], st["k"], st["v"], oT, sq, E_b, E2_b, biasb, qk_all, vv_all, tail)
                ss_pair(1)

            rg = stat[:, 224:256]
            rstd_from_ss(ssacc, rg, 32, 1.0 / 512, [("ssacc",)], [("rg",)])
            S.barrier()
            A.reset(ep_mark)
            tA2 = A.alloc([2, D], F32)
            tB2 = A.alloc([4, D], F32)
            jk2 = A.alloc([2, D], BF16)
            ecol = {}

            def e0(tb):
                pa = 0 if tb % 2 == 0 else 4

                def mmo(e):
                    ins = None
                    for g in range(2):
                        for dh in range(2):
                            for t in range(4):
                                ins = e.matmul(bank(pa + 2 * g + dh), lhsT=oT[:, 4 * g + t, tb * 128:(tb + 1) * 128],
                                               rhs=wout[:, 4 * g + t, dh * 512:(dh + 1) * 512], start=(t == 0), stop=(t == 3))
                    return ins
                S.op("tensor", mmo, reads=[("oT", t, tb // 4) for t in range(8)] + [("wout", t) for t in range(8)],
                     excl=PB(pa, pa + 1, pa + 2, pa + 3))

            def e1(tb):
                pa = 0 if tb % 2 == 0 else 4
                s4 = tb % 4
                S.op("scalar", lambda e: e.activation(out=tB2[:, s4, :], in_=bank(pa, 2), func=AF.Copy, scale=rg[:, tb:tb + 1]),
                     reads=[("rg",)], writes=[("tB2", s4)], excl=PB(pa, pa + 1))

            def e2(tb):
                pa = 0 if tb % 2 == 0 else 4
                s4 = tb % 4
                S.op("vector", lambda e: e.scalar_tensor_tensor(out=tB2[:, s4, :], in0=bank(pa + 2, 2), scalar=rg[:, 16 + tb:17 + tb], in1=tB2[:, s4, :],
                                                                op0=ALU.mult, op1=ALU.add),
                     reads=[("rg",), ("tB2", s4)], writes=[("tB2", s4)], excl=PB(pa + 2, pa + 3))

            def e3(tb):
                sl, s4 = tb % 2, tb % 4
                c = stat_cols(2)
                ecol[tb] = c
                S.op("scalar", lambda e: e.activation(out=jk2[:, sl, :], in_=tB2[:, s4, :], func=AF.Square, accum_out=stat[:, c:c + 1]),
                     reads=[("tB2", s4)], writes=[("jk2", sl), ("stat", c)])
                S.op("scalar", lambda e: e.activation(out=stat[:, c + 1:c + 2], in_=stat[:, c:c + 1], func=AF.Sqrt, scale=1.0 / D, bias=EPS),
                     reads=[("stat", c)], writes=[("stat", c + 1)])

            def e4(tb):
                sl, s4 = tb % 2, tb % 4
                c = ecol[tb]
                S.op("vector", lambda e: e.reciprocal(out=stat[:, c + 1:c + 2], in_=stat[:, c + 1:c + 2]), reads=[("stat", c + 1)], writes=[("stat", c + 1)])
                S.op("vector", lambda e: e.scalar_tensor_tensor(out=tA2[:, sl, :], in0=tB2[:, s4, :], scalar=stat[:, c + 1:c + 2], in1=gbuf[:, 1, :],
                                                                op0=ALU.mult, op1=ALU.mult),
                     reads=[("tB2", s4), ("stat", c + 1), ("gbuf", 1)], writes=[("tA2", sl)])
                S.op("vector", lambda e: e.tensor_tensor(out=h[:, tb, :], in0=h[:, tb, :], in1=tA2[:, sl, :], op=ALU.add),
                     reads=[("tA2", sl), ("h", tb)], writes=[("h", tb)])
                if tb < 8:
                    sq_stat(tb, 112 + tb)

            est = [e0, e1, e2, e3, e4]
            for t in range(NTB + len(est) - 1):
                for d in reversed(range(len(est))):
                    if 0 <= t - d < NTB:
                        est[d](t - d)

        def sb_attention(plist, oT, sq, e_b, sp_b, g_b, a_b, NW):
            items = []
            NPP = 40
            for P in plist:
                lanes = [[], []]
                for lane, Qs in ((0, (3, 0)), (1, (2, 1))):
                    for Q in Qs:
                        bs = list(range(4 * Q + 3, -1, -1))
                        for bi, b in enumerate(bs):
                            r = b - 4 * Q
                            c0 = 128 * r if r >= 0 else 0
                            lanes[lane].append(dict(Q=Q, b=b, c0=c0, n=512 - c0, diag=(r >= 0), first=(bi == 0), last=(bi == len(bs) - 1),
                                                    cb=2 + 2 * lane, lane=lane, lane_first=(Q == Qs[0]), P=P))
                assert len(lanes[0]) == len(lanes[1]) == NPP // 2
                for k in range(NPP // 2):
                    for lane in range(2):
                        items.append(lanes[lane][k])
            N = len(items)
            for i, it in enumerate(items):
                it["w"] = i % NW
                it["g"] = i % 3
            deferred = {}
            z2 = ps[:, 0:1024].rearrange("p (s c) -> p s c", s=2)

            def C2(it):
                return ps[:, it["cb"] * 512:(it["cb"] + 2) * 512].rearrange("p (s c) -> p s c", s=2)

            def P1(it):
                def f(e):
                    b, Q, c0, n = it["b"], it["Q"], it["c0"], it["n"]
                    qT, kT = it["P"]["q"], it["P"]["k"]
                    ins = None
                    for s_ in range(2):
                        ins = e.matmul(bank(s_)[:, 0:n], lhsT=kT[:, b * 128:(b + 1) * 128],
                                       rhs=qT[:, s_, Q * 512 + c0:(Q + 1) * 512], start=True, stop=not it["diag"])
                        if it["diag"]:
                            ins = e.matmul(bank(s_)[:, 0:n], lhsT=ident, rhs=sbmask[:, 0:n], start=False, stop=True)
                    return ins
                S.op("tensor", f, reads=it["P"]["qk"] + [("cst",)], excl=PB(0, 1))

            def A12(it):
                n, w = it["n"], it["w"]
                S.op("scalar", lambda e: e.activation(out=e_b[:, w, :, 0:n], in_=z2[:, :, 0:n], func=AF.Exp),
                     writes=[("e", w)], excl=PB(0, 1))
                S.op("scalar", lambda e: e.activation(out=sp_b[:, w, :, 0:n], in_=e_b[:, w, :, 0:n], func=AF.Ln, bias=1.0),
                     reads=[("e", w)], writes=[("sp", w)])

            def P2(it):
                n, w, c0, cb = it["n"], it["w"], it["c0"], it["cb"]
                if it["first"]:
                    S.op("vector", lambda e: e.memset(bank(cb, 2), 0.0), excl=PB(cb, cb + 1))

                def f(e):
                    ins = None
                    for s_ in range(2):
                        ins = e.matmul(bank(cb + s_)[:, c0:512], lhsT=tri, rhs=sp_b[:, w, s_, 0:n], start=False, stop=False, skip_group_check=True)
                    return ins
                S.op("tensor", f, reads=[("sp", w), ("cst",)], excl=PB(cb, cb + 1))

            def A3(it):
                n, c0, cb, g = it["n"], it["c0"], it["cb"], it["g"]
                S.op("scalar", lambda e: e.activation(out=g_b[:, g, :, 0:n], in_=C2(it)[:, :, c0:512], func=AF.Exp, scale=-1.0),
                     writes=[("G", g)], excl=PB(cb, cb + 1))

            def P3(it):
                n, w, c0, cb = it["n"], it["w"], it["c0"], it["cb"]
                if it["last"]:
                    return

                def f(e):
                    ins = None
                    for s_ in range(2):
                        ins = e.matmul(bank(cb + s_)[:, c0:512], lhsT=ctri, rhs=sp_b[:, w, s_, 0:n], start=False, stop=False, skip_group_check=True)
                    return ins
                S.op("tensor", f, reads=[("sp", w), ("cst",)], excl=PB(cb, cb + 1))

            def D1(it):
                n, w, g = it["n"], it["w"], it["g"]
                S.op("vector", lambda e: e.tensor_tensor(out=a_b[:, g, :, 0:n], in0=e_b[:, w, :, 0:n], in1=g_b[:, g, :, 0:n], op=ALU.mult),
                     reads=[("e", w), ("G", g)], writes=[("A", g)])

            oacc = [tmpA[:, 0:512], tmpB[:, 0:512]]

            def P4(it):
                n, c0, g, b = it["n"], it["c0"], it["g"], it["b"]
                vv = it["P"]["v"]

                def f(e):
                    ins = e.matmul(bank(6)[:, c0:512], lhsT=vv[:, 0, b, :], rhs=a_b[:, g, 0, 0:n], start=True, stop=False)
                    ins = e.matmul(bank(6)[:, c0:512], lhsT=vv[:, 1, b, :], rhs=a_b[:, g, 1, 0:n], start=False, stop=True)
                    return ins
                S.op("tensor", f, reads=[("A", g)] + it["P"]["vvk"], excl=PB(6))

            def ACC(it):
                c0, lane, Q, pair = it["c0"], it["lane"], it["Q"], it["P"]["pair"]
                oa = oacc[lane]
                if it["first"]:
                    S.op("vector", lambda e: e.memset(oa, 0.0), writes=[("oacc", lane)])
                S.op("vector", lambda e: e.tensor_tensor(out=oa[:, c0:512], in0=bank(6)[:, c0:512], in1=oa[:, c0:512], op=ALU.add),
                     reads=[("oacc", lane)], writes=[("oacc", lane)], excl=PB(6))
                if it["last"]:
                    S.op("vector", lambda e: e.tensor_tensor(out=sq[:, Q * 512:(Q + 1) * 512], in0=oa, in1=oa, op=ALU.mult),
                         reads=[("oacc", lane)], writes=[("sq", Q)])
                    S.op("vector", lambda e: e.tensor_scalar(out=oT[:, pair, Q * 512:(Q + 1) * 512], in0=oa, scalar1=gout[:, pair:pair + 1], scalar2=None, op0=ALU.mult),
                         reads=[("oacc", lane), ("gout",)], writes=[("oT", pair, Q)])

            def at(j, fn):
                if 0 <= j < N:
                    fn(items[j])

            for i, it in enumerate(items):
                it["idx"] = i
            extra = []
            for i in range(-2, N + 3):
                if i >= 0 and i % NPP == 0 and i // NPP < len(plist):
                    while extra:
                        extra.pop(0)()
                    plist[i // NPP]["start"]()
                    extra = plist[i // NPP]["extra"]()
                if i >= NPP and i % NPP == 3:
                    plist[i // NPP - 1]["finish"]()
                at(i - 2, P4)
                at(i - 2, ACC)
                at(i - 1, A3)
                at(i - 1, P3)
                at(i - 1, D1)
                at(i, P2)
                at(i + 1, A12)
                at(i + 2, P1)
                if i >= 0 and extra:
                    extra.pop(0)()
            while extra:
                extra.pop(0)()
            plist[-1]["finish"]()

        def chunk_attention(pair, qT, kT, vv, oT, sq, E_b, E2_b, biasb, qk_all, vv_all, tail):
            items = []
            for m in range(NTB):
                for s in range(2):
                    items.append(dict(m=m, s=s, ro=64 * s, hh=2 * (pair - 4) + s, dmin=max(0, 4 - m)))
            n_it = len(items)
            for i, itm in enumerate(items):
                itm["z"] = i % 2
                itm["ob"] = 4 + (itm["m"] // 4) % 2
                itm["db"] = 6 + (itm["m"] // 4) % 2

            def zcols(it, lo, hi):
                base = 1024 * it["z"]
                return ps[:, base + lo: base + hi]

            def c1(it):
                def f(e, it=it):
                    ro, m, hh, dmin = it["ro"], it["m"], it["hh"], it["dmin"]
                    ins = None
                    for dl in range(dmin, 5):
                        kb = m - 4 + dl
                        ins = e.matmul(zcols(it, dl * 128, (dl + 1) * 128), lhsT=kT[:, kb * 128:(kb + 1) * 128],
                                       rhs=qT[:, it["s"], m * 128:(m + 1) * 128], start=True, stop=True)
                    return ins
                S.op("tensor", f, reads=qk_all, excl=PB(2 * it["z"], 2 * it["z"] + 1))

            def c2(it):
                lo, z = it["dmin"] * 128, it["z"]
                S.op("scalar", lambda e: e.activation(out=E_b[:, z, lo:640], in_=zcols(it, lo, 640), func=AF.Exp),
                     writes=[("E", z)], excl=PB(2 * z, 2 * z + 1))

            def c2b(it):
                lo, z, hh, z3 = it["dmin"] * 128, it["z"], it["hh"], it["idx"] % 3
                S.op("vector", lambda e: e.tensor_tensor(out=E2_b[:, z3, lo:640], in0=E_b[:, z, lo:640], in1=biasb[:, hh, lo:640], op=ALU.mult),
                     reads=[("E", z), ("biasb",)], writes=[("E2", z3)])

            def c3(it):
                def f(e, it=it):
                    ro, m, dmin, z = it["ro"], it["m"], it["dmin"], it["idx"] % 3
                    cg = (m % 4) * 128
                    ins = None
                    for dl in range(dmin, 5):
                        kb = m - 4 + dl
                        s_ = it["s"]
                        fl = dict(start=(dl == dmin), stop=(dl == 4)) if s_ == 0 else dict(start=False, stop=False, skip_group_check=True)
                        ins = e.matmul(bank(it["ob"])[:, cg:cg + 128], lhsT=vv[:, s_, kb, :], rhs=E2_b[:, z, dl * 128:(dl + 1) * 128], **fl)
                        ins = e.matmul(bank(it["db"])[:, cg:cg + 128], lhsT=ones_h[s_], rhs=E2_b[:, z, dl * 128:(dl + 1) * 128], **fl)
                    return ins
                S.op("tensor", f, reads=[("E2", it["idx"] % 3), ("cst",)] + vv_all, excl=PB(it["ob"], it["db"]))
                if it["m"] % 4 == 3 and it["s"] == 1:
                    Qc = it["m"] // 4
                    ob, db = it["ob"], it["db"]
                    hs = Qc % 2

                    def evac():
                        S.op("scalar", lambda e: e.activation(out=tmpA[:, hs * 512:(hs + 1) * 512], in_=bank(db), func=AF.Ln),
                             writes=[("tmpA", hs)], excl=PB(db))
                        S.op("scalar", lambda e: e.activation(out=tmpA[:, hs * 512:(hs + 1) * 512], in_=tmpA[:, hs * 512:(hs + 1) * 512], func=AF.Exp, scale=-1.0),
                             reads=[("tmpA", hs)], writes=[("tmpA", hs)])
                        S.op("vector", lambda e: e.tensor_tensor(out=tmpB[:, hs * 512:(hs + 1) * 512], in0=bank(ob), in1=tmpA[:, hs * 512:(hs + 1) * 512], op=ALU.mult),
                             reads=[("tmpA", hs)], writes=[("tmpB", hs)], excl=PB(ob))

                    def evac2():
                        S.op("vector", lambda e: e.tensor_tensor(out=sq[:, Qc * 512:(Qc + 1) * 512], in0=tmpB[:, hs * 512:(hs + 1) * 512],
                                                                 in1=tmpB[:, hs * 512:(hs + 1) * 512], op=ALU.mult),
                             reads=[("tmpB", hs)], writes=[("sq", Qc)])

                    def evac3():
                        S.op("vector", lambda e: e.tensor_scalar(out=oT[:, pair, Qc * 512:(Qc + 1) * 512], in0=tmpB[:, hs * 512:(hs + 1) * 512],
                                                                 scalar1=gout[:, pair:pair + 1], scalar2=None, op0=ALU.mult),
                             reads=[("tmpB", hs), ("gout",)], writes=[("oT", pair, Qc)])
                    deferred[it["idx"] + 4 + 2] = evac
                    deferred[it["idx"] + 4 + 3] = evac2
                    deferred[it["idx"] + 4 + 4] = evac3

            deferred = {}
            for i, itm in enumerate(items):
                itm["idx"] = i
            stages = [(0, c1), (1, c2), (2, c2b), (4, c3)]
            for t in range(n_it + 9):
                if t in deferred:
                    deferred.pop(t)()
                for d, fn in reversed(stages):
                    i = t - d
                    if 0 <= i < n_it:
                        fn(items[i])
                if tail:
                    nq = len(tail)
                    if n_it <= t < n_it + 4 and nq > 4:
                        tail.pop(0)()
                        tail.pop(0)()
                    elif t >= n_it + 4:
                        tail.pop(0)()
                        if tail:
                            tail.pop(0)()
            while tail:
                tail.pop(0)()
            assert not deferred

        def ple():
            S.barrier()
            A.reset(base_mark)
            wgate_ = A.alloc([8, 1024], BF16)
            wgate = xT
            wple = wgu[:, 0, :, :]
            hT = A.alloc([2, 8, 128], BF16)
            pb = A.alloc([2, 256], BF16)
            pT = A.alloc([4, 2, 128], BF16)
            ob = A.alloc([2, D], F32)
            tS = A.alloc([6, D], F32)
            tT = A.alloc([2, D], F32)
            jk = A.alloc([2, D], BF16)
            scol = {}

            def p0(tb):
                sl = tb % 2
                S.dma("gpsimd", [lambda e: e.dma_start(out=pb[:, sl, :], in_=p_d[tb * 128:(tb + 1) * 128, :])],
                      "p%d" % sl, writes=[("pb", sl)])
                S.op("scalar", lambda e: e.activation(out=xn_tm[:, sl, :], in_=h[:, tb, :], func=AF.Copy),
                     reads=[("h", tb)], writes=[("xn", sl)])

            def p1(tb):
                sl = tb % 2

                def tr(e):
                    ins = None
                    for kc in range(8):
                        ins = e.transpose(out=bank_bf(0)[:, kc * 128:(kc + 1) * 128], in_=xn_tm[:, sl, kc * 128:(kc + 1) * 128], identity=ident)
                    for c in range(2):
                        ins = e.transpose(out=bank_bf(1)[:, c * 128:(c + 1) * 128], in_=pb[:, sl, c * 128:(c + 1) * 128], identity=ident)
                    return ins
                S.op("tensor", tr, reads=[("xn", sl), ("pb", sl), ("cst",)], excl=PB(0, 1))

            def p2(tb):
                sl = tb % 2
                S.op("vector", lambda e: e.tensor_copy(out=hT[:, sl, :, :], in_=bank_bf(0).rearrange("p (k t) -> p k t", k=8)),
                     writes=[("hT", sl)], excl=PB(0))
                S.op("vector", lambda e: e.tensor_copy(out=pT[:, tb % 4, :, :], in_=bank_bf(1)[:, 0:256].rearrange("p (k t) -> p k t", k=2)),
                     writes=[("pT", tb % 4)], excl=PB(1))

            def p3(tb):
                sl = tb % 2
                gbk = 2 if sl == 0 else 4

                def mmg(e):
                    ins = None
                    for dh in range(2):
                        for kc in range(8):
                            ins = e.matmul(bank(gbk + dh), lhsT=hT[:, sl, kc, :], rhs=wgate[:, kc, dh * 512:(dh + 1) * 512], start=(kc == 0), stop=(kc == 7))
                    return ins
                S.op("tensor", mmg, reads=[("hT", sl)] + [("wgate", j) for j in range(8)], excl=PB(gbk, gbk + 1))

            def p4(tb):
                sl, s6 = tb % 2, tb % 6
                gbk = 2 if sl == 0 else 4
                S.op("scalar", lambda e: e.activation(out=tS[:, s6, :], in_=bank(gbk, 2), func=AF.Exp, scale=-1.0), writes=[("tS", s6)], excl=PB(gbk, gbk + 1))
                S.op("scalar", lambda e: e.activation(out=tS[:, s6, :], in_=tS[:, s6, :], func=AF.Ln, bias=1.0), reads=[("tS", s6)], writes=[("tS", s6)])
                S.op("scalar", lambda e: e.activation(out=tS[:, s6, :], in_=tS[:, s6, :], func=AF.Exp, scale=-1.0), reads=[("tS", s6)], writes=[("tS", s6)])

            def p5(tb):
                s4 = tb % 4

                def mmp(e):
                    ins = None
                    for dh in range(2):
                        for c in range(2):
                            ins = e.matmul(bank(6 + dh), lhsT=pT[:, s4, c, :], rhs=wple[:, c, dh * 512:(dh + 1) * 512], start=(c == 0), stop=(c == 1))
                    return ins
                S.op("tensor", mmp, reads=[("pT", s4), ("wple",)], excl=PB(6, 7))

            def p6(tb):
                s6 = tb % 6
                S.op("vector", lambda e: e.tensor_tensor(out=tS[:, s6, :], in0=bank(6, 2), in1=tS[:, s6, :], op=ALU.mult),
                     reads=[("tS", s6)], writes=[("tS", s6)], excl=PB(6, 7))

            def p7(tb):
                sl, s6 = tb % 2, tb % 6
                c = stat_cols(2)
                scol[tb] = c
                S.op("scalar", lambda e: e.activation(out=jk[:, sl, :], in_=tS[:, s6, :], func=AF.Square, accum_out=stat[:, c:c + 1]),
                     reads=[("tS", s6)], writes=[("jk", sl), ("stat", c)])
                rstd_from_ss(stat[:, c:c + 1], stat[:, c + 1:c + 2], 1, 1.0 / D, [("stat", c)], [("stat", c + 1)], True)

            def p8(tb):
                sl, s6 = tb % 2, tb % 6
                c = scol[tb]
                S.op("vector", lambda e: e.scalar_tensor_tensor(out=tT[:, sl, :], in0=tS[:, s6, :], scalar=stat[:, c + 1:c + 2], in1=gbuf[:, 0, :],
                                                                op0=ALU.mult, op1=ALU.mult),
                     reads=[("tS", s6), ("stat", c + 1), ("gbuf", 0)], writes=[("tT", sl)])
                S.op("vector", lambda e: e.tensor_tensor(out=ob[:, sl, :], in0=h[:, tb, :], in1=tT[:, sl, :], op=ALU.add),
                     reads=[("tT", sl), ("h", tb)], writes=[("ob", sl)])
                S.dma("sync", [lambda e: e.dma_start(out=out_d[tb * 128:(tb + 1) * 128, :], in_=ob[:, sl, :])],
                      "out%d" % sl, reads=[("ob", sl)], writes=[("out", tb)])

            def run(fn, i):
                if 0 <= i < NTB:
                    fn(i)
            for t in range(NTB + 5):
                run(p0, t)
                run(p1, t)
                run(p8, t - 5)
                run(p7, t - 4)
                run(p6, t - 3)
                run(p4, t - 2)
                run(p5, t - 2)
                run(p3, t - 1)
                run(p2, t)

        ffn(0, G_FFN1_PRE, G_FFN1_POST)
        attention_prefetch()
        attention()
        ffn_prefetch(1, G_FFN2_PRE, G_FFN2_POST)
        ffn(1, G_FFN2_PRE, G_FFN2_POST)
        ple()
        S.finish(nc, block, final_waits=["out0", "out1"])
    return nc


def _slab_cols(w):
    K, Fd = w.shape
    return np.ascontiguousarray(w.reshape(K // 128, 128, Fd // 128, 128).transpose(2, 1, 0, 3).reshape(Fd // 128, 128, (K // 128) * 128))


def _rows_part(w):
    R, C = w.shape
    return np.ascontiguousarray(w.reshape(R // 128, 128, C).transpose(1, 0, 2).reshape(128, (R // 128) * C))


def _consts():
    c = np.zeros((128, 1280), np.float32)
    j = np.arange(128)
    c[:, 0:128] = np.eye(128, dtype=np.float32)
    c[:, 128:256] = (j[:, None] >= j[None, :])
    c[:, 256:384] = (j[:, None] < j[None, :])
    c[:, 384:512] = 1.0
    y = np.arange(512)
    c[:, 512:576] = 1.0
    c[:, 704:768] = 1.0
    c[:, 768:1280] = np.where(j[:, None] < y[None, :], 0.0, NEG)
    return c


def _bias_gather(rel_bias):
    sk = np.arange(128)[:, None, None]
    dl = np.arange(5)[None, :, None]
    tq = np.arange(128)[None, None, :]
    idx = np.clip(128 * (4 - dl) + tq - sk, -128, 128) + 128
    g = rel_bias[:, idx]
    g = g.transpose(1, 0, 2, 3)
    compact = np.ascontiguousarray(g[:, :, 3:5, :].reshape(128, 8 * 256)).astype(np.float32)
    const = np.ascontiguousarray(np.broadcast_to(rel_bias[None, :, 256], (128, 8))).astype(np.float32)
    return compact, const


_NC_CACHE = {}


def _prep_shared(inp):
    f = lambda a: np.asarray(a, dtype=np.float32)
    sh = {}
    sh["g7"] = np.ascontiguousarray(np.stack([f(inp[k])[0] for k in (
        "g_ffn1_pre", "g_ffn1_post", "g_mix_pre", "g_mix_post", "g_ffn2_pre", "g_ffn2_post", "g_ple_post")], 0))
    go = np.concatenate([f(inp["g_out_sb"])[0], f(inp["g_out_ch"])[0]])
    sh["gout"] = np.ascontiguousarray(go.reshape(8, 128).T)
    sh["cst"] = _consts()
    sh["biasq"], sh["bconst"] = _bias_gather(f(inp["rel_bias"])[0])
    sh["wg1"] = _slab_cols(f(inp["w_ffn1_gate"])[0])
    sh["wu1"] = _slab_cols(f(inp["w_ffn1_up"])[0])
    sh["wd1"] = _rows_part(f(inp["w_ffn1_down"])[0])
    sh["wg2"] = _slab_cols(f(inp["w_ffn2_gate"])[0])
    sh["wu2"] = _slab_cols(f(inp["w_ffn2_up"])[0])
    sh["wd2"] = _rows_part(f(inp["w_ffn2_down"])[0])
    win = f(inp["w_in"])[0]
    slabs = _slab_cols(win)
    wl = np.zeros((8, 128, 3 * 1024), np.float32)
    for pair in range(8):
        g, pp = pair // 4, pair % 4
        for m in range(3):
            wl[pair, :, m * 1024:(m + 1) * 1024] = slabs[g * 12 + m * 4 + pp]
    sh["win"] = wl
    sh["wout"] = _rows_part(f(inp["w_out"])[0])
    sh["wgate"] = _rows_part(f(inp["w_ple_gate"])[0])
    sh["wple"] = _rows_part(f(inp["w_ple_proj"])[0])
    return sh


def kernel(**inputs):
    x = np.asarray(inputs["x"], dtype=np.float32)
    p = np.asarray(inputs["p"], dtype=np.float32)
    sh = _prep_shared(inputs)
    if "nc" not in _NC_CACHE:
        _NC_CACHE["nc"] = build()
    nc = _NC_CACHE["nc"]
    n = 8
    in_maps = []
    for c in range(n):
        m = dict(sh)
        m["x"] = np.ascontiguousarray(x[c])
        m["p"] = np.ascontiguousarray(p[0, c])
        in_maps.append(m)
    res = run_bass_kernel_spmd(nc, in_maps, core_ids=list(range(n)))
    return np.stack([np.asarray(r["out"], dtype=np.float32) for r in res.results], 0)
```

```python
import numpy as np
import concourse.bass as bass
import concourse.mybir as mybir
from concourse.bass_utils import run_bass_kernel_spmd

F32 = mybir.dt.float32
BF16 = mybir.dt.bfloat16
AF = mybir.ActivationFunctionType
ALU = mybir.AluOpType

ENGS = ("sync", "scalar", "vector", "gpsimd", "tensor")


class Sched:
    def __init__(self):
        self.prog = {e: [] for e in ENGS}
        self.cnt = {}
        self.known = {e: {} for e in ENGS}
        self.last_w = {}
        self.readers = {}
        self.last_x = {}

    def _emit(self, eng, fn, reads, writes, excl, sem, inc):
        toks = []
        for r in reads:
            t = self.last_w.get(r)
            if t is not None:
                toks.append(t)
        for w in writes:
            t = self.last_w.get(w)
            if t is not None:
                toks.append(t)
            toks.extend(self.readers.get(w, ()))
        for x in excl:
            t = self.last_x.get(x)
            if t is not None and not (t[2] == eng and eng == "tensor"):
                toks.append((t[0], t[1], t[2]))
        kn = self.known[eng]
        need = {}
        for (s, v, src) in toks:
            if src == eng and eng == "tensor" and s == "tensor":
                continue
            if kn.get(s, 0) < v and need.get(s, 0) < v:
                need[s] = v
        for s, v in need.items():
            kn[s] = v
        self.cnt[sem] = self.cnt.get(sem, 0) + inc
        tok = (sem, self.cnt[sem], eng if sem == eng else "dma:" + sem)
        self.prog[eng].append((tuple(need.items()), fn, sem, inc))
        for r in reads:
            self.readers.setdefault(r, []).append(tok)
        for w in writes:
            self.last_w[w] = tok
            self.readers[w] = []
        for x in excl:
            self.last_x[x] = (tok[0], tok[1], eng)
        return tok

    def op(self, eng, fn, reads=(), writes=(), excl=()):
        return self._emit(eng, fn, reads, writes, excl, eng, 1)

    def dma(self, eng, fns, sem, reads=(), writes=()):
        return self._emit(eng, list(fns), reads, writes, (), sem, 16 * len(fns))

    def finish(self, nc, block, final_waits=()):
        names = set(ENGS) | {s for e in ENGS for (_, _, s, _) in self.prog[e] if s is not None}
        sems = {}
        import contextlib
        self._stack = contextlib.ExitStack()
        for n in sorted(names):
            sems[n] = self._stack.enter_context(nc.semaphore("s_" + n))
        finals = [(s, self.cnt[s]) for s in final_waits]
        allfin = [(e, self.cnt.get(e, 0)) for e in ENGS if self.cnt.get(e, 0) > 0]

        def make(engname):
            def f(e):
                for (waits, fn, s, inc) in self.prog[engname]:
                    for (ws, wv) in waits:
                        e.wait_ge(sems[ws], wv)
                    if fn is None:
                        continue
                    if isinstance(fn, list):
                        for f1 in fn:
                            f1(e).then_inc(sems[s], 16)
                    else:
                        fn(e).then_inc(sems[s], inc)
                if engname == "sync":
                    for (s, v) in finals + allfin:
                        e.wait_ge(sems[s], v)
            return f

        block.sync(make("sync"))
        block.scalar(make("scalar"))
        block.vector(make("vector"))
        block.gpsimd(make("gpsimd"))
        block.tensor(make("tensor"))

    def barrier(self):
        for eng in ENGS:
            kn = self.known[eng]
            need = {s: v for s, v in self.cnt.items() if kn.get(s, 0) < v}
            for s, v in need.items():
                kn[s] = v
            if need:
                self.prog[eng].append((tuple(need.items()), None, None, 0))


T = 2048
D = 1024
NTB = 16
FF = 2816
NFC = 22
EPS = 1e-6
NEG = -30000.0

G_FFN1_PRE, G_FFN1_POST, G_MIX_PRE, G_MIX_POST, G_FFN2_PRE, G_FFN2_POST, G_PLE = range(7)


class Arena:
    def __init__(self, ap, nbytes):
        self.ap = ap
        self.nbytes = nbytes
        self.off = 0

    def mark(self):
        return self.off

    def reset(self, m):
        self.off = m

    def alloc(self, shape, dt):
        n = int(np.prod(shape))
        nb = n * (4 if dt == F32 else 2)
        nb_al = (nb + 31) // 32 * 32
        assert self.off + nb_al <= self.nbytes, (self.off, nb_al, self.nbytes)
        a = self.ap[:, self.off // 2: (self.off + nb) // 2]
        self.off += nb_al
        if dt == F32:
            a = a.bitcast(F32)
        if len(shape) == 2:
            a = a.rearrange("p (a b) -> p a b", a=shape[0])
        elif len(shape) == 3:
            a = a.rearrange("p (a b c) -> p a b c", a=shape[0], b=shape[1])
        return a


ARENA_BYTES = 212800


def build(debug=False):
    nc = bass.Bass("TRN2", target_bir_lowering=False)
    dt_in = lambda n, s: nc.dram_tensor(n, s, F32, kind="ExternalInput").ap()
    x_d = dt_in("x", [T, D])
    p_d = dt_in("p", [T, 256])
    g7_d = dt_in("g7", [7, D])
    gout_d = dt_in("gout", [128, 8])
    cst_d = dt_in("cst", [128, 6 * 128 + 512])
    biasq_d = dt_in("biasq", [128, 8 * 256])
    bconst_d = dt_in("bconst", [128, 8])
    wg_d = [dt_in("wg1", [NFC, 128, 1024]), dt_in("wg2", [NFC, 128, 1024])]
    wu_d = [dt_in("wu1", [NFC, 128, 1024]), dt_in("wu2", [NFC, 128, 1024])]
    wd_d = [dt_in("wd1", [128, NFC * 1024]), dt_in("wd2", [128, NFC * 1024])]
    win_d = dt_in("win", [8, 128, 3 * 1024])
    wout_d = dt_in("wout", [128, 8 * 1024])
    wgate_d = dt_in("wgate", [128, 8 * 1024])
    wple_d = dt_in("wple", [128, 2 * 1024])
    out_d = nc.dram_tensor("out", [T, D], F32, kind="ExternalOutput").ap()

    S = Sched()
    with (
        nc.sbuf_tensor("arena", [128, ARENA_BYTES // 2], BF16) as arena_t,
        nc.psum_tensor("ps", [128, 4096], F32) as ps,
        nc.Block() as block,
    ):
        A = Arena(arena_t, ARENA_BYTES)
        h = A.alloc([NTB, D], F32)
        cst = A.alloc([6, 128], BF16)
        ident, tri, ctri, ones = cst[:, 0, :], cst[:, 1, :], cst[:, 2, :], cst[:, 3, :]
        ones_h = [cst[:, 4, :], cst[:, 5, :]]
        sbmask = A.alloc([512], BF16)
        gbuf = A.alloc([2, D], F32)
        gout = A.alloc([8], F32)
        bconst = A.alloc([8], F32)
        stat = A.alloc([256], F32)
        junk = A.alloc([D], BF16)
        tmpA = A.alloc([D], F32)
        tmpB = A.alloc([D], F32)
        xn_tm = A.alloc([2, D], BF16)
        base_mark = A.mark()

        def bank(b, n=1):
            return ps[:, b * 512:(b + n) * 512]

        def bank_bf(b):
            return ps[:, b * 512:(b + 1) * 512].bitcast(BF16)

        PB = lambda *bs: [("ps", b) for b in bs]
        TP = lambda ro: ({"tile_position": (0, ro)} if ro else {})

        stat_ctr = [0]

        def stat_cols(n):
            c = stat_ctr[0]
            if c + n > 112:
                c = 0
            stat_ctr[0] = c + n
            return c

        S.dma("gpsimd", [lambda e: e.dma_start(out=cst.rearrange("p a b -> p (a b)"), in_=cst_d[:, 0:768]),
                         lambda e: e.dma_start(out=sbmask, in_=cst_d[:, 768:1280])],
              "cst", writes=[("cst",)])
        S.dma("sync", [lambda e: e.dma_start(out=gout, in_=gout_d), lambda e: e.dma_start(out=bconst, in_=bconst_d)], "gout", writes=[("gout",), ("bconst",)])

        def load_gains(i0, i1, half1):
            S.dma("sync", [lambda e: e.dma_start(out=gbuf[:, 0, :], in_=g7_d[i0:i0 + 1, :].broadcast_to([128, D]))],
                  "g0", writes=[("gbuf", 0)])
            if i1 is not None:
                S.dma("sync", [lambda e: e.dma_start(out=gbuf[:, 1, :], in_=g7_d[i1:i1 + 1, :].broadcast_to([128, D]))],
                      "g1", writes=[("gbuf", 1)])
                if half1:
                    S.op("vector", lambda e: e.tensor_scalar(out=gbuf[:, 1, :], in0=gbuf[:, 1, :], scalar1=0.5, scalar2=None, op0=ALU.mult),
                         reads=[("gbuf", 1)], writes=[("gbuf", 1)])

        def rstd_from_ss(ss_ap, r_ap, n, inv_n, rkeys_in, wkeys_out, use_lnexp=False):
            if not use_lnexp:
                S.op("scalar", lambda e: e.activation(out=r_ap, in_=ss_ap, func=AF.Sqrt, scale=inv_n, bias=EPS),
                     reads=rkeys_in, writes=wkeys_out)
                S.op("vector", lambda e: e.reciprocal(out=r_ap, in_=r_ap), reads=wkeys_out, writes=wkeys_out)
            else:
                S.op("vector", lambda e: e.tensor_scalar(out=r_ap, in0=ss_ap, scalar1=inv_n, scalar2=EPS, op0=ALU.mult, op1=ALU.add),
                     reads=rkeys_in, writes=wkeys_out)
                S.op("scalar", lambda e: e.activation(out=r_ap, in_=r_ap, func=AF.Ln), reads=wkeys_out, writes=wkeys_out)
                S.op("scalar", lambda e: e.activation(out=r_ap, in_=r_ap, func=AF.Exp, scale=-0.5), reads=wkeys_out, writes=wkeys_out)

        TRB = 6
        xn_ctr = [0]

        def sq_stat(tb, col):
            S.op("scalar", lambda e: e.activation(out=junk, in_=h[:, tb, :], func=AF.Square, accum_out=stat[:, col:col + 1]),
                 reads=[("h", tb)], writes=[("junk",), ("stat", col)])

        def norm_stats(tbs):
            n = len(tbs)
            c0 = stat_cols(2 * n)
            for i, tb in enumerate(tbs):
                S.op("scalar", lambda e, tb=tb, i=i: e.activation(out=junk, in_=h[:, tb, :], func=AF.Square, accum_out=stat[:, c0 + i:c0 + i + 1]),
                     reads=[("h", tb)], writes=[("junk",), ("stat", c0 + i)])
            rstd_from_ss(stat[:, c0:c0 + n], stat[:, c0 + n:c0 + 2 * n], n, 1.0 / D,
                         [("stat", c0 + i) for i in range(n)], [("stat", c0 + n + i) for i in range(n)])
            return c0

        def nx_a(tb, i, c0, n):
            sl = xn_ctr[0] % 2
            xn_ctr[0] += 1
            S.op("vector", lambda e: e.scalar_tensor_tensor(
                out=xn_tm[:, sl, :], in0=h[:, tb, :], scalar=stat[:, c0 + n + i:c0 + n + i + 1], in1=gbuf[:, 0, :],
                op0=ALU.mult, op1=ALU.mult),
                reads=[("h", tb), ("stat", c0 + n + i), ("gbuf", 0)], writes=[("xn", sl)])
            return sl

        def norm_xpose_one(tb, i, c0, n, dstT, dst_key, col, trb, evac_eng, koff=0):
            sl = nx_a(tb, i, c0, n)
            nx_b(sl, dstT, dst_key, col, trb, evac_eng, koff)

        def nx_b(sl, dstT, dst_key, col, trb, evac_eng, koff=0):
            tps = bank_bf(trb)

            def tr(e):
                ins = None
                for kc in range(8):
                    ins = e.transpose(out=tps[:, kc * 128:(kc + 1) * 128], in_=xn_tm[:, sl, kc * 128:(kc + 1) * 128], identity=ident)
                return ins
            S.op("tensor", tr, reads=[("xn", sl), ("cst",)], excl=PB(trb))
            if evac_eng == "scalar":
                S.op("scalar", lambda e: e.activation(out=dstT[:, :, col:col + 128], in_=tps.rearrange("p (k t) -> p k t", k=8), func=AF.Copy),
                     writes=[(dst_key, col // 128 + koff)], excl=PB(trb))
            else:
                S.op("vector", lambda e: e.tensor_copy(out=dstT[:, :, col:col + 128], in_=tps.rearrange("p (k t) -> p k t", k=8)),
                     writes=[(dst_key, col // 128 + koff)], excl=PB(trb))

        def norm_transpose(tbs, dstT, dst_key, col_of, trbs=(6, 7), batched=True, c0=None):
            n = len(tbs)
            if batched:
                if c0 is None:
                    c0 = norm_stats(tbs)
                for i, tb in enumerate(tbs):
                    norm_xpose_one(tb, i, c0, n, dstT, dst_key, col_of(tb), trbs[i % len(trbs)], "scalar" if i % 2 == 0 else "vector")
            else:
                for i, tb in enumerate(tbs):
                    c1 = norm_stats([tb])
                    norm_xpose_one(tb, 0, c1, 1, dstT, dst_key, col_of(tb), trbs[i % len(trbs)], "scalar" if i % 2 == 0 else "vector")

        def post_norm_residual(src_ap, src_reads, src_excl, tb, gslot, use_lnexp=False, out_ap=None, out_key=None):
            c = stat_cols(2)
            S.op("scalar", lambda e: e.activation(out=junk, in_=src_ap, func=AF.Square, accum_out=stat[:, c:c + 1]),
                 reads=src_reads, writes=[("junk",), ("stat", c)], excl=src_excl)
            rstd_from_ss(stat[:, c:c + 1], stat[:, c + 1:c + 2], 1, 1.0 / D, [("stat", c)], [("stat", c + 1)], use_lnexp)
            S.op("vector", lambda e: e.scalar_tensor_tensor(out=tmpA, in0=src_ap, scalar=stat[:, c + 1:c + 2], in1=gbuf[:, gslot, :],
                                                            op0=ALU.mult, op1=ALU.mult),
                 reads=list(src_reads) + [("stat", c + 1), ("gbuf", gslot)], writes=[("tmpA",)], excl=src_excl)
            if out_ap is None:
                S.op("vector", lambda e: e.tensor_tensor(out=h[:, tb, :], in0=h[:, tb, :], in1=tmpA, op=ALU.add),
                     reads=[("tmpA",), ("h", tb)], writes=[("h", tb)])
            else:
                S.op("vector", lambda e: e.tensor_tensor(out=out_ap, in0=h[:, tb, :], in1=tmpA, op=ALU.add),
                     reads=[("tmpA",), ("h", tb)], writes=[out_key])

        wgu_ctr = [0]

        A.reset(base_mark)
        xT = A.alloc([8, 1024], BF16)
        hid = A.alloc([NFC, 1024], BF16)
        wd = A.alloc([NFC, 1024], BF16)
        wgu = A.alloc([2, 2, 1024], BF16)
        sg = A.alloc([2, 512], BF16)
        A.reset(base_mark)
        pre = {}

        def load_wgu(li, fc):
            sl = wgu_ctr[0] % 2
            wgu_ctr[0] += 1
            S.dma("gpsimd", [lambda e: e.dma_start(out=wgu[:, sl, 0, :], in_=wg_d[li][fc]),
                             lambda e: e.dma_start(out=wgu[:, sl, 1, :], in_=wu_d[li][fc])],
                  "wgu%d" % sl, writes=[("wgu", sl)])
            return sl

        def ffn_prefetch(li, gpre, gpost):
            load_gains(gpre, gpost, True)
            pre["gains%d" % li] = True
            rstd_from_ss(stat[:, 112:120], stat[:, 120:128], 8, 1.0 / D,
                         [("stat", 112 + i) for i in range(8)], [("stat", 120 + i) for i in range(8)])
            pre["stats%d" % li] = 112
            pre["wgu%d" % li] = [load_wgu(li, 0), load_wgu(li, 1)]

        def ple_prefetch():
            for j in range(0, 8, 2):
                S.dma("gpsimd", [lambda e, j=j: e.dma_start(out=xT[:, j:j + 2, :], in_=wgate_d[:, j * 1024:(j + 2) * 1024].rearrange("p (a b) -> p a b", a=2))],
                      "wgate%d" % (j // 2), writes=[("wgate", j), ("wgate", j + 1)] + [("xT", i) for i in range(8)])
            S.dma("gpsimd", [lambda e: e.dma_start(out=wgu[:, 0, :, :], in_=wple_d.rearrange("p (a b) -> p a b", a=2))], "wple",
                  writes=[("wple",), ("wgu", 0)])
            load_gains(G_PLE, None, False)

        def ffn(li, gpre, gpost):
            S.barrier()
            def load_x(tb):
                S.dma("sync", [lambda e: e.dma_start(out=h[:, tb, :], in_=x_d[tb * 128:(tb + 1) * 128, :])],
                      "x%d" % tb, writes=[("h", tb)])
            if li == 0:
                for tb in range(8):
                    load_x(tb)
            if not pre.get("gains%d" % li):
                load_gains(gpre, gpost, True)
            if li == 0:
                for tb in range(8, NTB):
                    load_x(tb)

            def load_wd(j):
                S.dma("gpsimd", [lambda e: e.dma_start(out=wd[:, j:j + 2, :].rearrange("p a b -> p (a b)"),
                                                      in_=wd_d[li][:, j * 1024:(j + 2) * 1024])],
                      "wd%d" % (j // 2), writes=[("wd", j), ("wd", j + 1)])
            it = 0
            for hf in range(2):
                tbs = list(range(8 * hf, 8 * hf + 8))
                if hf == 0:
                    if "stats%d" % li in pre:
                        c0p = pre["stats%d" % li]
                        for i, tb in enumerate(tbs):
                            norm_xpose_one(tb, i, c0p, 8, xT, "xT", (tb % 8) * 128, 6 + i % 2, "scalar" if i % 2 == 0 else "vector")
                    elif li == 0:
                        cs = stat_cols(16)
                        sls = {}
                        for i in range(10):
                            if i < 8:
                                sq_stat(i, cs + i)
                            if 1 <= i <= 8:
                                j = i - 1
                                rstd_from_ss(stat[:, cs + j:cs + j + 1], stat[:, cs + 8 + j:cs + 9 + j], 1, 1.0 / D, [("stat", cs + j)], [("stat", cs + 8 + j)])
                                sls[j] = nx_a(j, j, cs, 8)
                            if i >= 2:
                                j = i - 2
                                nx_b(sls[j], xT, "xT", j * 128, 6 + j % 2, "vector" if j < 6 else "scalar")
                    else:
                        norm_transpose(tbs, xT, "xT", lambda tb: (tb % 8) * 128)
                for fc in range(NFC):
                    if hf == 0 and 2 <= fc < 13:
                        load_wd(2 * (fc - 2))
                    if hf == 0 and fc < 2 and pre.get("wgu%d" % li):
                        sl = pre["wgu%d" % li][fc]
                    else:
                        sl = load_wgu(li, fc)
                    for tt in range(2):
                        gb, ub = it % 2, 2 + it % 2
                        sgs = it % 2
                        it += 1

                        def mm(e, sl=sl, tt=tt, gb=gb, ub=ub):
                            ins = None
                            for m, bk in ((0, gb), (1, ub)):
                                for kc in range(8):
                                    ins = e.matmul(bank(bk), lhsT=wgu[:, sl, m, kc * 128:(kc + 1) * 128],
                                                   rhs=xT[:, kc, tt * 512:(tt + 1) * 512], start=(kc == 0), stop=(kc == 7))
                            return ins
                        S.op("tensor", mm, reads=[("wgu", sl)] + [("xT", tt * 4 + q) for q in range(4)], excl=PB(gb, ub))
                        S.op("scalar", lambda e, gb=gb, sgs=sgs: e.activation(out=sg[:, sgs, :], in_=bank(gb), func=AF.Silu),
                             writes=[("sg", sgs)], excl=PB(gb))
                        S.op("vector", lambda e, ub=ub, sgs=sgs, fc=fc, tt=tt: e.tensor_tensor(
                            out=hid[:, fc, tt * 512:(tt + 1) * 512], in0=bank(ub), in1=sg[:, sgs, :], op=ALU.mult),
                            reads=[("sg", sgs)], writes=[("hid", fc, tt)], excl=PB(ub))
                if hf == 0:
                    nx_c0 = norm_stats(list(range(8, 16)))
                if hf == 1 and li == 1:
                    ple_prefetch()
                if hf == 1 and li == 0:
                    S.dma("sync", [lambda e: e.dma_start(out=gbuf[:, 0, :], in_=g7_d[G_MIX_PRE:G_MIX_PRE + 1, :].broadcast_to([128, D]))],
                          "g0", writes=[("gbuf", 0)])
                    for tb_ in range(8):
                        sq_stat(tb_, 128 + tb_)
                    rstd_from_ss(stat[:, 128:136], stat[:, 144:152], 8, 1.0 / D,
                                 [("stat", 128 + i) for i in range(8)], [("stat", 144 + i) for i in range(8)])
                for tbl in range(8):
                    tb = 8 * hf + tbl
                    fb = 4 if tbl % 2 == 0 else 6

                    def mmd(e, tbl=tbl, fb=fb):
                        ins = None
                        for dh in range(2):
                            for fc in range(NFC):
                                ins = e.matmul(bank(fb + dh), lhsT=hid[:, fc, tbl * 128:(tbl + 1) * 128],
                                               rhs=wd[:, fc, dh * 512:(dh + 1) * 512], start=(fc == 0), stop=(fc == NFC - 1))
                        return ins
                    if hf == 0:
                        nsl = nx_a(8 + tbl, tbl, nx_c0, 8)
                    elif li == 0:
                        nsl = nx_a(tbl, tbl, 128, 16)
                    S.op("tensor", mmd, reads=[("hid", fc, tbl // 4) for fc in range(NFC)] + [("wd", fc) for fc in range(NFC)],
                         excl=PB(fb, fb + 1))
                    if hf == 0 or li == 0:
                        nx_b(nsl, xT, "xT", tbl * 128, tbl % 2, "scalar" if tbl % 2 == 0 else "vector")
                    post_norm_residual(bank(fb, 2), [], PB(fb, fb + 1), tb, 1)
                    if li == 0 and hf == 1:
                        sq_stat(tb, 128 + tb)

        def attention_prefetch():
            S.dma("sync", [lambda e: e.dma_start(out=gbuf[:, 1, :], in_=g7_d[G_MIX_POST:G_MIX_POST + 1, :].broadcast_to([128, D]))],
                  "g1", writes=[("gbuf", 1)])
            rstd_from_ss(stat[:, 136:144], stat[:, 152:160], 8, 1.0 / D,
                         [("stat", 136 + i) for i in range(8)], [("stat", 152 + i) for i in range(8)])
            pre["attn_stats"] = 128

        def attention():
            S.barrier()
            A.reset(base_mark)
            uTh = [A.alloc([8, 1024], BF16), A.alloc([8, 1024], BF16)]

            def uTs(kc, c0_, c1_):
                hf_ = c0_ // 1024
                return uTh[hf_][:, kc, c0_ - 1024 * hf_:c1_ - 1024 * hf_]
            oT = A.alloc([8, T], BF16)
            ep_mark = A.mark()
            qT = A.alloc([2, T], BF16)
            kT = A.alloc([T], BF16)
            vv = A.alloc([2, NTB, 128], BF16)
            wqkv = A.alloc([3, 1024], BF16)
            sq = A.alloc([T], BF16)
            NW = 3
            wk_mark = A.mark()
            e_b = A.alloc([NW, 2, 512], BF16)
            sp_b = A.alloc([NW, 2, 512], BF16)
            g_b = A.alloc([3, 2, 512], BF16)
            a_b = A.alloc([3, 2, 512], BF16)
            A.reset(wk_mark)
            E_b = A.alloc([2, 640], BF16)
            E2_b = A.alloc([3, 640], BF16)
            biasb = A.alloc([8, 640], BF16)
            ssacc = stat[:, 192:224]
            S.op("vector", lambda e: e.memset(ssacc, 0.0), writes=[("ssacc",)])
            S.op("vector", lambda e: e.memset(qT[64:128, 0, :], 0.0), writes=[("qT", i) for i in range(4)])
            S.op("vector", lambda e: e.memset(qT[0:64, 1, :], 0.0), writes=[("qT", i) for i in range(4)])
            S.op("vector", lambda e: e.memset(vv[:, 0, :, 64:128], 0.0), writes=[("vv", i) for i in range(4)])
            S.op("vector", lambda e: e.memset(vv[:, 1, :, 0:64], 0.0), writes=[("vv", i) for i in range(4)])
            for i_, tb_ in enumerate(range(8, NTB)):
                norm_xpose_one(tb_, tb_, 128, 16, uTh[1], "uT", (tb_ % 8) * 128, 6 + i_ % 2, "scalar" if i_ % 2 == 0 else "vector", koff=8)
            uT_all = [("uT", i) for i in range(NTB)]
            wout = uTh[0]

            qT1 = oT[:, 4:6, :]
            vv1 = oT[:, 6:8, :].rearrange("p s (t c) -> p s t c", c=128)
            kT1 = xn_tm.rearrange("p a b -> p (a b)")
            gb0 = gbuf[:, 0, :].bitcast(BF16)
            sets = [
                dict(q=qT, k=kT, v=vv, w=[wqkv[:, 0, :], wqkv[:, 1, :], wqkv[:, 2, :]], kq="qT", kk="kT", kv="vv",
                     kw=[("wqkv",)], sem="win"),
                dict(q=qT1, k=kT1, v=vv1, w=[junk, gb0[:, 0:1024], gb0[:, 1024:2048]], kq="qT1", kk="kT1", kv="vv1",
                     kw=[("junk",), ("gbuf", 0), ("xn", 0), ("xn", 1)], sem="win1"),
            ]
            st1 = sets[1]
            S.op("vector", lambda e: e.memset(st1["q"][64:128, 0, :], 0.0), writes=[("qT1", i) for i in range(4)])
            S.op("vector", lambda e: e.memset(st1["q"][0:64, 1, :], 0.0), writes=[("qT1", i) for i in range(4)])
            S.op("vector", lambda e: e.memset(st1["v"][:, 0, :, 64:128], 0.0), writes=[("vv1", i) for i in range(4)])
            S.op("vector", lambda e: e.memset(st1["v"][:, 1, :, 0:64], 0.0), writes=[("vv1", i) for i in range(4)])

            def issue_w(pair, st):
                S.dma("gpsimd", [lambda e, m=m: e.dma_start(out=st["w"][m], in_=win_d[pair][:, m * 1024:(m + 1) * 1024]) for m in range(3)],
                      st["sem"], writes=st["kw"])

            def proj_closures(pair, st, overlapped):
                out = []
                units = [(m, t) for m in (0, 1) for t in range(4)] + [(2, t) for t in range(4)]
                for ui, (m, t) in enumerate(units):
                    if overlapped:
                        bk = 7
                    else:
                        bk = (t % 2) if m < 2 else 2 + t % 2
                    if m < 2:
                        mm = [(None, kc) for kc in range(8)]
                        splits = [3, 3, 2]
                        rkeys = st["kw"] + uT_all[t * 4:t * 4 + 4]
                    else:
                        mm = [(j, kc) for j in range(4) for kc in range(8)]
                        splits = [11, 11, 10]
                        rkeys = st["kw"] + uT_all[t * 4:t * 4 + 4]
                    if not overlapped:
                        splits = [len(mm)]
                    pos = 0
                    for si, n_ in enumerate(splits):
                        part = mm[pos:pos + n_]
                        pos += n_
                        last = (si == len(splits) - 1)

                        def emit(part=part, last=last, m=m, t=t, bk=bk, rkeys=rkeys):
                            def f(e):
                                ins = None
                                for (j, kc) in part:
                                    if m < 2:
                                        ins = e.matmul(bank(bk), lhsT=st["w"][m][:, kc * 128:(kc + 1) * 128], rhs=uTs(kc, t * 512, (t + 1) * 512),
                                                       start=(kc == 0), stop=(kc == 7))
                                    else:
                                        tb = 4 * t + j
                                        ins = e.matmul(bank(bk)[:, j * 128:(j + 1) * 128], lhsT=uTs(kc, tb * 128, (tb + 1) * 128),
                                                       rhs=st["w"][2][:, kc * 128:(kc + 1) * 128], start=(kc == 0), stop=(kc == 7))
                                return ins
                            S.op("tensor", f, reads=rkeys, excl=PB(bk))
                            if not last:
                                return
                            if m == 0:
                                for hh_, (lo, hi) in enumerate(((0, 64), (64, 128))):
                                    if overlapped:
                                        S.op("vector", lambda e, hh_=hh_, lo=lo, hi=hi: e.tensor_scalar(
                                            out=st["q"][lo:hi, hh_, t * 512:(t + 1) * 512], in0=bank(bk)[lo:hi, :], scalar1=0.125, scalar2=None, op0=ALU.mult),
                                            writes=[(st["kq"], t)], excl=PB(bk))
                                    else:
                                        S.op("scalar", lambda e, hh_=hh_, lo=lo, hi=hi: e.activation(
                                            out=st["q"][lo:hi, hh_, t * 512:(t + 1) * 512], in_=bank(bk)[lo:hi, :], func=AF.Copy, scale=0.125),
                                            writes=[(st["kq"], t)], excl=PB(bk))
                            elif m == 1:
                                S.op("vector", lambda e: e.tensor_copy(out=st["k"][:, t * 512:(t + 1) * 512], in_=bank(bk)),
                                     writes=[(st["kk"], t)], excl=PB(bk))
                            else:
                                src = bank(bk).rearrange("p (a b) -> p a b", a=4)
                                if overlapped:
                                    S.op("vector", lambda e: e.tensor_copy(out=st["v"][:, 0, 4 * t:4 * t + 4, 0:64], in_=src[:, :, 0:64]),
                                         writes=[(st["kv"], t)], excl=PB(bk))
                                else:
                                    S.op("scalar", lambda e: e.activation(out=st["v"][:, 0, 4 * t:4 * t + 4, 0:64], in_=src[:, :, 0:64], func=AF.Copy),
                                         writes=[(st["kv"], t)], excl=PB(bk))
                                S.op("vector", lambda e: e.tensor_copy(out=st["v"][:, 1, 4 * t:4 * t + 4, 64:128], in_=src[:, :, 64:128]),
                                     writes=[(st["kv"], t)], excl=PB(bk))
                        out.append(emit)
                return out

            def ss_pair(grp, bk=0):
                def mms(e):
                    ins = None
                    for tb in range(NTB):
                        ins = e.matmul(bank(bk)[:, tb:tb + 1], lhsT=sq[:, tb * 128:(tb + 1) * 128], rhs=ones[:, 0:1], start=True, stop=True)
                    return ins
                S.op("tensor", mms, reads=[("sq", i) for i in range(4)] + [("cst",)], excl=PB(bk))
                S.op("vector", lambda e: e.tensor_tensor(out=ssacc[:, grp * 16:(grp + 1) * 16], in0=bank(bk)[:, 0:16],
                                                         in1=ssacc[:, grp * 16:(grp + 1) * 16], op=ALU.add),
                     reads=[("ssacc",)], writes=[("ssacc",)], excl=PB(bk))

            def keys_of(st):
                return ([(st["kq"], i) for i in range(4)] + [(st["kk"], i) for i in range(4)], [(st["kv"], i) for i in range(4)])

            issue_w(0, sets[0])
            for c in proj_closures(0, sets[0], False):
                c()
            plist = []
            for pair in range(4):
                st = sets[pair % 2]
                nst = sets[(pair + 1) % 2]
                qk_all, vv_all = keys_of(st)
                plist.append(dict(pair=pair, q=st["q"], k=st["k"], v=st["v"], qk=qk_all, vvk=vv_all,
                                  start=(lambda pair=pair, nst=nst: issue_w(pair + 1, nst)),
                                  extra=(lambda pair=pair, nst=nst: proj_closures(pair + 1, nst, True)),
                                  finish=(lambda: ss_pair(0, 6))))
            sb_attention(plist, oT, sq, e_b, sp_b, g_b, a_b, NW)
            for pair in range(4, 8):
                st = sets[0]
                if pair == 4:
                    S.barrier()
                    S.dma("gpsimd", [lambda e: e.dma_start(out=biasb[:, :, 384:640], in_=biasq_d.rearrange("p (a b) -> p a b", a=8))], "biasq", writes=[("biasb",)])
                    for hh in range(8):
                        S.op("vector", lambda e, hh=hh: e.tensor_scalar(out=biasb[:, hh, 0:384], in0=sbmask[:, 0:384], scalar1=0.0, scalar2=bconst[:, hh:hh + 1],
                                                                       op0=ALU.mult, op1=ALU.add),
                             reads=[("bconst",), ("cst",), ("biasb",)], writes=[("biasb",)])
                    for hh in range(8):
                        S.op("vector", lambda e, hh=hh: e.memset(biasb[64:128, hh, 512:576], NEG), reads=[("biasb",)], writes=[("biasb",)])
                        S.op("vector", lambda e, hh=hh: e.memset(biasb[0:64, hh, 64:128], NEG), reads=[("biasb",)], writes=[("biasb",)])
                    S.op("scalar", lambda e: e.activation(out=biasb.rearrange("p a b -> p (a b)"), in_=biasb.rearrange("p a b -> p (a b)"), func=AF.Exp),
                         reads=[("biasb",)], writes=[("biasb",)])
                if pair == 7:
                    for j in range(0, 8, 2):
                        S.dma("gpsimd", [lambda e, j=j: e.dma_start(out=wout[:, j:j + 2, :],
                                                                   in_=wout_d[:, j * 1024:(j + 2) * 1024].rearrange("p (a b) -> p a b", a=2))],
                              "wout%d" % (j // 2), writes=[("wout", j), ("wout", j + 1)] + uT_all)
                qk_all, vv_all = keys_of(st)
                tail = []
                if pair < 7:
                    issue_w(pair + 1, st)
                    tail = proj_closures(pair + 1, st, False)
                chunk_attention(pair, st["q"], st["k"], st["v"], oT, sq, E_b, E2_b, biasb, qk_all, vv_all, tail)
                ss_pair(1)

            rg = stat[:, 224:256]
            rstd_from_ss(ssacc, rg, 32, 1.0 / 512, [("ssacc",)], [("rg",)])
            S.barrier()
            A.reset(ep_mark)
            tA2 = A.alloc([2, D], F32)
            tB2 = A.alloc([4, D], F32)
            jk2 = A.alloc([2, D], BF16)
            ecol = {}

            def e0(tb):
                pa = 0 if tb % 2 == 0 else 4

                def mmo(e):
                    ins = None
                    for g in range(2):
                        for dh in range(2):
                            for t in range(4):
                                ins = e.matmul(bank(pa + 2 * g + dh), lhsT=oT[:, 4 * g + t, tb * 128:(tb + 1) * 128],
                                               rhs=wout[:, 4 * g + t, dh * 512:(dh + 1) * 512], start=(t == 0), stop=(t == 3))
                    return ins
                S.op("tensor", mmo, reads=[("oT", t, tb // 4) for t in range(8)] + [("wout", t) for t in range(8)],
                     excl=PB(pa, pa + 1, pa + 2, pa + 3))

            def e1(tb):
                pa = 0 if tb % 2 == 0 else 4
                s4 = tb % 4
                S.op("scalar", lambda e: e.activation(out=tB2[:, s4, :], in_=bank(pa, 2), func=AF.Copy, scale=rg[:, tb:tb + 1]),
                     reads=[("rg",)], writes=[("tB2", s4)], excl=PB(pa, pa + 1))

            def e2(tb):
                pa = 0 if tb % 2 == 0 else 4
                s4 = tb % 4
                S.op("vector", lambda e: e.scalar_tensor_tensor(out=tB2[:, s4, :], in0=bank(pa + 2, 2), scalar=rg[:, 16 + tb:17 + tb], in1=tB2[:, s4, :],
                                                                op0=ALU.mult, op1=ALU.add),
                     reads=[("rg",), ("tB2", s4)], writes=[("tB2", s4)], excl=PB(pa + 2, pa + 3))

            def e3(tb):
                sl, s4 = tb % 2, tb % 4
                c = stat_cols(2)
                ecol[tb] = c
                S.op("scalar", lambda e: e.activation(out=jk2[:, sl, :], in_=tB2[:, s4, :], func=AF.Square, accum_out=stat[:, c:c + 1]),
                     reads=[("tB2", s4)], writes=[("jk2", sl), ("stat", c)])
                S.op("scalar", lambda e: e.activation(out=stat[:, c + 1:c + 2], in_=stat[:, c:c + 1], func=AF.Sqrt, scale=1.0 / D, bias=EPS),
                     reads=[("stat", c)], writes=[("stat", c + 1)])

            def e4(tb):
                sl, s4 = tb % 2, tb % 4
                c = ecol[tb]
                S.op("vector", lambda e: e.reciprocal(out=stat[:, c + 1:c + 2], in_=stat[:, c + 1:c + 2]), reads=[("stat", c + 1)], writes=[("stat", c + 1)])
                S.op("vector", lambda e: e.scalar_tensor_tensor(out=tA2[:, sl, :], in0=tB2[:, s4, :], scalar=stat[:, c + 1:c + 2], in1=gbuf[:, 1, :],
                                                                op0=ALU.mult, op1=ALU.mult),
                     reads=[("tB2", s4), ("stat", c + 1), ("gbuf", 1)], writes=[("tA2", sl)])
                S.op("vector", lambda e: e.tensor_tensor(out=h[:, tb, :], in0=h[:, tb, :], in1=tA2[:, sl, :], op=ALU.add),
                     reads=[("tA2", sl), ("h", tb)], writes=[("h", tb)])
                if tb < 8:
                    sq_stat(tb, 112 + tb)

            est = [e0, e1, e2, e3, e4]
            for t in range(NTB + len(est) - 1):
                for d in reversed(range(len(est))):
                    if 0 <= t - d < NTB:
                        est[d](t - d)

        def sb_attention(plist, oT, sq, e_b, sp_b, g_b, a_b, NW):
            items = []
            NPP = 40
            for P in plist:
                lanes = [[], []]
                for lane, Qs in ((0, (3, 0)), (1, (2, 1))):
                    for Q in Qs:
                        bs = list(range(4 * Q + 3, -1, -1))
                        for bi, b in enumerate(bs):
                            r = b - 4 * Q
                            c0 = 128 * r if r >= 0 else 0
                            lanes[lane].append(dict(Q=Q, b=b, c0=c0, n=512 - c0, diag=(r >= 0), first=(bi == 0), last=(bi == len(bs) - 1),
                                                    cb=2 + 2 * lane, lane=lane, lane_first=(Q == Qs[0]), P=P))
                assert len(lanes[0]) == len(lanes[1]) == NPP // 2
                for k in range(NPP // 2):
                    for lane in range(2):
                        items.append(lanes[lane][k])
            N = len(items)
            for i, it in enumerate(items):
                it["w"] = i % NW
                it["g"] = i % 3
            deferred = {}
            z2 = ps[:, 0:1024].rearrange("p (s c) -> p s c", s=2)

            def C2(it):
                return ps[:, it["cb"] * 512:(it["cb"] + 2) * 512].rearrange("p (s c) -> p s c", s=2)

            def P1(it):
                def f(e):
                    b, Q, c0, n = it["b"], it["Q"], it["c0"], it["n"]
                    qT, kT = it["P"]["q"], it["P"]["k"]
                    ins = None
                    for s_ in range(2):
                        ins = e.matmul(bank(s_)[:, 0:n], lhsT=kT[:, b * 128:(b + 1) * 128],
                                       rhs=qT[:, s_, Q * 512 + c0:(Q + 1) * 512], start=True, stop=not it["diag"])
                        if it["diag"]:
                            ins = e.matmul(bank(s_)[:, 0:n], lhsT=ident, rhs=sbmask[:, 0:n], start=False, stop=True)
                    return ins
                S.op("tensor", f, reads=it["P"]["qk"] + [("cst",)], excl=PB(0, 1))

            def A12(it):
                n, w = it["n"], it["w"]
                S.op("scalar", lambda e: e.activation(out=e_b[:, w, :, 0:n], in_=z2[:, :, 0:n], func=AF.Exp),
                     writes=[("e", w)], excl=PB(0, 1))
                S.op("scalar", lambda e: e.activation(out=sp_b[:, w, :, 0:n], in_=e_b[:, w, :, 0:n], func=AF.Ln, bias=1.0),
                     reads=[("e", w)], writes=[("sp", w)])

            def P2(it):
                n, w, c0, cb = it["n"], it["w"], it["c0"], it["cb"]
                if it["first"]:
                    S.op("vector", lambda e: e.memset(bank(cb, 2), 0.0), excl=PB(cb, cb + 1))

                def f(e):
                    ins = None
                    for s_ in range(2):
                        ins = e.matmul(bank(cb + s_)[:, c0:512], lhsT=tri, rhs=sp_b[:, w, s_, 0:n], start=False, stop=False, skip_group_check=True)
                    return ins
                S.op("tensor", f, reads=[("sp", w), ("cst",)], excl=PB(cb, cb + 1))

            def A3(it):
                n, c0, cb, g = it["n"], it["c0"], it["cb"], it["g"]
                S.op("scalar", lambda e: e.activation(out=g_b[:, g, :, 0:n], in_=C2(it)[:, :, c0:512], func=AF.Exp, scale=-1.0),
                     writes=[("G", g)], excl=PB(cb, cb + 1))

            def P3(it):
                n, w, c0, cb = it["n"], it["w"], it["c0"], it["cb"]
                if it["last"]:
                    return

                def f(e):
                    ins = None
                    for s_ in range(2):
                        ins = e.matmul(bank(cb + s_)[:, c0:512], lhsT=ctri, rhs=sp_b[:, w, s_, 0:n], start=False, stop=False, skip_group_check=True)
                    return ins
                S.op("tensor", f, reads=[("sp", w), ("cst",)], excl=PB(cb, cb + 1))

            def D1(it):
                n, w, g = it["n"], it["w"], it["g"]
                S.op("vector", lambda e: e.tensor_tensor(out=a_b[:, g, :, 0:n], in0=e_b[:, w, :, 0:n], in1=g_b[:, g, :, 0:n], op=ALU.mult),
                     reads=[("e", w), ("G", g)], writes=[("A", g)])

            oacc = [tmpA[:, 0:512], tmpB[:, 0:512]]

            def P4(it):
                n, c0, g, b = it["n"], it["c0"], it["g"], it["b"]
                vv = it["P"]["v"]

                def f(e):
                    ins = e.matmul(bank(6)[:, c0:512], lhsT=vv[:, 0, b, :], rhs=a_b[:, g, 0, 0:n], start=True, stop=False)
                    ins = e.matmul(bank(6)[:, c0:512], lhsT=vv[:, 1, b, :], rhs=a_b[:, g, 1, 0:n], start=False, stop=True)
                    return ins
                S.op("tensor", f, reads=[("A", g)] + it["P"]["vvk"], excl=PB(6))

            def ACC(it):
                c0, lane, Q, pair = it["c0"], it["lane"], it["Q"], it["P"]["pair"]
                oa = oacc[lane]
                if it["first"]:
                    S.op("vector", lambda e: e.memset(oa, 0.0), writes=[("oacc", lane)])
                S.op("vector", lambda e: e.tensor_tensor(out=oa[:, c0:512], in0=bank(6)[:, c0:512], in1=oa[:, c0:512], op=ALU.add),
                     reads=[("oacc", lane)], writes=[("oacc", lane)], excl=PB(6))
                if it["last"]:
                    S.op("vector", lambda e: e.tensor_tensor(out=sq[:, Q * 512:(Q + 1) * 512], in0=oa, in1=oa, op=ALU.mult),
                         reads=[("oacc", lane)], writes=[("sq", Q)])
                    S.op("vector", lambda e: e.tensor_scalar(out=oT[:, pair, Q * 512:(Q + 1) * 512], in0=oa, scalar1=gout[:, pair:pair + 1], scalar2=None, op0=ALU.mult),
                         reads=[("oacc", lane), ("gout",)], writes=[("oT", pair, Q)])

            def at(j, fn):
                if 0 <= j < N:
                    fn(items[j])

            for i, it in enumerate(items):
                it["idx"] = i
            extra = []
            for i in range(-2, N + 3):
                if i >= 0 and i % NPP == 0 and i // NPP < len(plist):
                    while extra:
                        extra.pop(0)()
                    plist[i // NPP]["start"]()
                    extra = plist[i // NPP]["extra"]()
                if i >= NPP and i % NPP == 3:
                    plist[i // NPP - 1]["finish"]()
                at(i - 2, P4)
                at(i - 2, ACC)
                at(i - 1, A3)
                at(i - 1, P3)
                at(i - 1, D1)
                at(i, P2)
                at(i + 1, A12)
                at(i + 2, P1)
                if i >= 0 and extra:
                    extra.pop(0)()
            while extra:
                extra.pop(0)()
            plist[-1]["finish"]()

        def chunk_attention(pair, qT, kT, vv, oT, sq, E_b, E2_b, biasb, qk_all, vv_all, tail):
            items = []
            for m in range(NTB):
                for s in range(2):
                    items.append(dict(m=m, s=s, ro=64 * s, hh=2 * (pair - 4) + s, dmin=max(0, 4 - m)))
            n_it = len(items)
            for i, itm in enumerate(items):
                itm["z"] = i % 2
                itm["ob"] = 4 + (itm["m"] // 4) % 2
                itm["db"] = 6 + (itm["m"] // 4) % 2

            def zcols(it, lo, hi):
                base = 1024 * it["z"]
                return ps[:, base + lo: base + hi]

            def c1(it):
                def f(e, it=it):
                    ro, m, hh, dmin = it["ro"], it["m"], it["hh"], it["dmin"]
                    ins = None
                    for dl in range(dmin, 5):
                        kb = m - 4 + dl
                        ins = e.matmul(zcols(it, dl * 128, (dl + 1) * 128), lhsT=kT[:, kb * 128:(kb + 1) * 128],
                                       rhs=qT[:, it["s"], m * 128:(m + 1) * 128], start=True, stop=True)
                    return ins
                S.op("tensor", f, reads=qk_all, excl=PB(2 * it["z"], 2 * it["z"] + 1))

            def c2(it):
                lo, z = it["dmin"] * 128, it["z"]
                S.op("scalar", lambda e: e.activation(out=E_b[:, z, lo:640], in_=zcols(it, lo, 640), func=AF.Exp),
                     writes=[("E", z)], excl=PB(2 * z, 2 * z + 1))

            def c2b(it):
                lo, z, hh, z3 = it["dmin"] * 128, it["z"], it["hh"], it["idx"] % 3
                S.op("vector", lambda e: e.tensor_tensor(out=E2_b[:, z3, lo:640], in0=E_b[:, z, lo:640], in1=biasb[:, hh, lo:640], op=ALU.mult),
                     reads=[("E", z), ("biasb",)], writes=[("E2", z3)])

            def c3(it):
                def f(e, it=it):
                    ro, m, dmin, z = it["ro"], it["m"], it["dmin"], it["idx"] % 3
                    cg = (m % 4) * 128
                    ins = None
                    for dl in range(dmin, 5):
                        kb = m - 4 + dl
                        s_ = it["s"]
                        fl = dict(start=(dl == dmin), stop=(dl == 4)) if s_ == 0 else dict(start=False, stop=False, skip_group_check=True)
                        ins = e.matmul(bank(it["ob"])[:, cg:cg + 128], lhsT=vv[:, s_, kb, :], rhs=E2_b[:, z, dl * 128:(dl + 1) * 128], **fl)
                        ins = e.matmul(bank(it["db"])[:, cg:cg + 128], lhsT=ones_h[s_], rhs=E2_b[:, z, dl * 128:(dl + 1) * 128], **fl)
                    return ins
                S.op("tensor", f, reads=[("E2", it["idx"] % 3), ("cst",)] + vv_all, excl=PB(it["ob"], it["db"]))
                if it["m"] % 4 == 3 and it["s"] == 1:
                    Qc = it["m"] // 4
                    ob, db = it["ob"], it["db"]
                    hs = Qc % 2

                    def evac():
                        S.op("scalar", lambda e: e.activation(out=tmpA[:, hs * 512:(hs + 1) * 512], in_=bank(db), func=AF.Ln),
                             writes=[("tmpA", hs)], excl=PB(db))
                        S.op("scalar", lambda e: e.activation(out=tmpA[:, hs * 512:(hs + 1) * 512], in_=tmpA[:, hs * 512:(hs + 1) * 512], func=AF.Exp, scale=-1.0),
                             reads=[("tmpA", hs)], writes=[("tmpA", hs)])
                        S.op("vector", lambda e: e.tensor_tensor(out=tmpB[:, hs * 512:(hs + 1) * 512], in0=bank(ob), in1=tmpA[:, hs * 512:(hs + 1) * 512], op=ALU.mult),
                             reads=[("tmpA", hs)], writes=[("tmpB", hs)], excl=PB(ob))

                    def evac2():
                        S.op("vector", lambda e: e.tensor_tensor(out=sq[:, Qc * 512:(Qc + 1) * 512], in0=tmpB[:, hs * 512:(hs + 1) * 512],
                                                                 in1=tmpB[:, hs * 512:(hs + 1) * 512], op=ALU.mult),
                             reads=[("tmpB", hs)], writes=[("sq", Qc)])

                    def evac3():
                        S.op("vector", lambda e: e.tensor_scalar(out=oT[:, pair, Qc * 512:(Qc + 1) * 512], in0=tmpB[:, hs * 512:(hs + 1) * 512],
                                                                 scalar1=gout[:, pair:pair + 1], scalar2=None, op0=ALU.mult),
                             reads=[("tmpB", hs), ("gout",)], writes=[("oT", pair, Qc)])
                    deferred[it["idx"] + 4 + 2] = evac
                    deferred[it["idx"] + 4 + 3] = evac2
                    deferred[it["idx"] + 4 + 4] = evac3

            deferred = {}
            for i, itm in enumerate(items):
                itm["idx"] = i
            stages = [(0, c1), (1, c2), (2, c2b), (4, c3)]
            for t in range(n_it + 9):
                if t in deferred:
                    deferred.pop(t)()
                for d, fn in reversed(stages):
                    i = t - d
                    if 0 <= i < n_it:
                        fn(items[i])
                if tail:
                    nq = len(tail)
                    if n_it <= t < n_it + 4 and nq > 4:
                        tail.pop(0)()
                        tail.pop(0)()
                    elif t >= n_it + 4:
                        tail.pop(0)()
                        if tail:
                            tail.pop(0)()
            while tail:
                tail.pop(0)()
            assert not deferred

        def ple():
            S.barrier()
            A.reset(base_mark)
            wgate_ = A.alloc([8, 1024], BF16)
            wgate = xT
            wple = wgu[:, 0, :, :]
            hT = A.alloc([2, 8, 128], BF16)
            pb = A.alloc([2, 256], BF16)
            pT = A.alloc([4, 2, 128], BF16)
            ob = A.alloc([2, D], F32)
            tS = A.alloc([6, D], F32)
            tT = A.alloc([2, D], F32)
            jk = A.alloc([2, D], BF16)
            scol = {}

            def p0(tb):
                sl = tb % 2
                S.dma("gpsimd", [lambda e: e.dma_start(out=pb[:, sl, :], in_=p_d[tb * 128:(tb + 1) * 128, :])],
                      "p%d" % sl, writes=[("pb", sl)])
                S.op("scalar", lambda e: e.activation(out=xn_tm[:, sl, :], in_=h[:, tb, :], func=AF.Copy),
                     reads=[("h", tb)], writes=[("xn", sl)])

            def p1(tb):
                sl = tb % 2

                def tr(e):
                    ins = None
                    for kc in range(8):
                        ins = e.transpose(out=bank_bf(0)[:, kc * 128:(kc + 1) * 128], in_=xn_tm[:, sl, kc * 128:(kc + 1) * 128], identity=ident)
                    for c in range(2):
                        ins = e.transpose(out=bank_bf(1)[:, c * 128:(c + 1) * 128], in_=pb[:, sl, c * 128:(c + 1) * 128], identity=ident)
                    return ins
                S.op("tensor", tr, reads=[("xn", sl), ("pb", sl), ("cst",)], excl=PB(0, 1))

            def p2(tb):
                sl = tb % 2
                S.op("vector", lambda e: e.tensor_copy(out=hT[:, sl, :, :], in_=bank_bf(0).rearrange("p (k t) -> p k t", k=8)),
                     writes=[("hT", sl)], excl=PB(0))
                S.op("vector", lambda e: e.tensor_copy(out=pT[:, tb % 4, :, :], in_=bank_bf(1)[:, 0:256].rearrange("p (k t) -> p k t", k=2)),
                     writes=[("pT", tb % 4)], excl=PB(1))

            def p3(tb):
                sl = tb % 2
                gbk = 2 if sl == 0 else 4

                def mmg(e):
                    ins = None
                    for dh in range(2):
                        for kc in range(8):
                            ins = e.matmul(bank(gbk + dh), lhsT=hT[:, sl, kc, :], rhs=wgate[:, kc, dh * 512:(dh + 1) * 512], start=(kc == 0), stop=(kc == 7))
                    return ins
                S.op("tensor", mmg, reads=[("hT", sl)] + [("wgate", j) for j in range(8)], excl=PB(gbk, gbk + 1))

            def p4(tb):
                sl, s6 = tb % 2, tb % 6
                gbk = 2 if sl == 0 else 4
                S.op("scalar", lambda e: e.activation(out=tS[:, s6, :], in_=bank(gbk, 2), func=AF.Exp, scale=-1.0), writes=[("tS", s6)], excl=PB(gbk, gbk + 1))
                S.op("scalar", lambda e: e.activation(out=tS[:, s6, :], in_=tS[:, s6, :], func=AF.Ln, bias=1.0), reads=[("tS", s6)], writes=[("tS", s6)])
                S.op("scalar", lambda e: e.activation(out=tS[:, s6, :], in_=tS[:, s6, :], func=AF.Exp, scale=-1.0), reads=[("tS", s6)], writes=[("tS", s6)])

            def p5(tb):
                s4 = tb % 4

                def mmp(e):
                    ins = None
                    for dh in range(2):
                        for c in range(2):
                            ins = e.matmul(bank(6 + dh), lhsT=pT[:, s4, c, :], rhs=wple[:, c, dh * 512:(dh + 1) * 512], start=(c == 0), stop=(c == 1))
                    return ins
                S.op("tensor", mmp, reads=[("pT", s4), ("wple",)], excl=PB(6, 7))

            def p6(tb):
                s6 = tb % 6
                S.op("vector", lambda e: e.tensor_tensor(out=tS[:, s6, :], in0=bank(6, 2), in1=tS[:, s6, :], op=ALU.mult),
                     reads=[("tS", s6)], writes=[("tS", s6)], excl=PB(6, 7))

            def p7(tb):
                sl, s6 = tb % 2, tb % 6
                c = stat_cols(2)
                scol[tb] = c
                S.op("scalar", lambda e: e.activation(out=jk[:, sl, :], in_=tS[:, s6, :], func=AF.Square, accum_out=stat[:, c:c + 1]),
                     reads=[("tS", s6)], writes=[("jk", sl), ("stat", c)])
                rstd_from_ss(stat[:, c:c + 1], stat[:, c + 1:c + 2], 1, 1.0 / D, [("stat", c)], [("stat", c + 1)], True)

            def p8(tb):
                sl, s6 = tb % 2, tb % 6
                c = scol[tb]
                S.op("vector", lambda e: e.scalar_tensor_tensor(out=tT[:, sl, :], in0=tS[:, s6, :], scalar=stat[:, c + 1:c + 2], in1=gbuf[:, 0, :],
                                                                op0=ALU.mult, op1=ALU.mult),
                     reads=[("tS", s6), ("stat", c + 1), ("gbuf", 0)], writes=[("tT", sl)])
                S.op("vector", lambda e: e.tensor_tensor(out=ob[:, sl, :], in0=h[:, tb, :], in1=tT[:, sl, :], op=ALU.add),
                     reads=[("tT", sl), ("h", tb)], writes=[("ob", sl)])
                S.dma("sync", [lambda e: e.dma_start(out=out_d[tb * 128:(tb + 1) * 128, :], in_=ob[:, sl, :])],
                      "out%d" % sl, reads=[("ob", sl)], writes=[("out", tb)])

            def run(fn, i):
                if 0 <= i < NTB:
                    fn(i)
            for t in range(NTB + 5):
                run(p0, t)
                run(p1, t)
                run(p8, t - 5)
                run(p7, t - 4)
                run(p6, t - 3)
                run(p4, t - 2)
                run(p5, t - 2)
                run(p3, t - 1)
                run(p2, t)

        ffn(0, G_FFN1_PRE, G_FFN1_POST)
        attention_prefetch()
        attention()
        ffn_prefetch(1, G_FFN2_PRE, G_FFN2_POST)
        ffn(1, G_FFN2_PRE, G_FFN2_POST)
        ple()
        S.finish(nc, block, final_waits=["out0", "out1"])
    return nc


def _slab_cols(w):
    K, Fd = w.shape
    return np.ascontiguousarray(w.reshape(K // 128, 128, Fd // 128, 128).transpose(2, 1, 0, 3).reshape(Fd // 128, 128, (K // 128) * 128))


def _rows_part(w):
    R, C = w.shape
    return np.ascontiguousarray(w.reshape(R // 128, 128, C).transpose(1, 0, 2).reshape(128, (R // 128) * C))


def _consts():
    c = np.zeros((128, 1280), np.float32)
    j = np.arange(128)
    c[:, 0:128] = np.eye(128, dtype=np.float32)
    c[:, 128:256] = (j[:, None] >= j[None, :])
    c[:, 256:384] = (j[:, None] < j[None, :])
    c[:, 384:512] = 1.0
    y = np.arange(512)
    c[:, 512:576] = 1.0
    c[:, 704:768] = 1.0
    c[:, 768:1280] = np.where(j[:, None] < y[None, :], 0.0, NEG)
    return c


def _bias_gather(rel_bias):
    sk = np.arange(128)[:, None, None]
    dl = np.arange(5)[None, :, None]
    tq = np.arange(128)[None, None, :]
    idx = np.clip(128 * (4 - dl) + tq - sk, -128, 128) + 128
    g = rel_bias[:, idx]
    g = g.transpose(1, 0, 2, 3)
    compact = np.ascontiguousarray(g[:, :, 3:5, :].reshape(128, 8 * 256)).astype(np.float32)
    const = np.ascontiguousarray(np.broadcast_to(rel_bias[None, :, 256], (128, 8))).astype(np.float32)
    return compact, const


_NC_CACHE = {}


def _prep_shared(inp):
    f = lambda a: np.asarray(a, dtype=np.float32)
    sh = {}
    sh["g7"] = np.ascontiguousarray(np.stack([f(inp[k])[0] for k in (
        "g_ffn1_pre", "g_ffn1_post", "g_mix_pre", "g_mix_post", "g_ffn2_pre", "g_ffn2_post", "g_ple_post")], 0))
    go = np.concatenate([f(inp["g_out_sb"])[0], f(inp["g_out_ch"])[0]])
    sh["gout"] = np.ascontiguousarray(go.reshape(8, 128).T)
    sh["cst"] = _consts()
    sh["biasq"], sh["bconst"] = _bias_gather(f(inp["rel_bias"])[0])
    sh["wg1"] = _slab_cols(f(inp["w_ffn1_gate"])[0])
    sh["wu1"] = _slab_cols(f(inp["w_ffn1_up"])[0])
    sh["wd1"] = _rows_part(f(inp["w_ffn1_down"])[0])
    sh["wg2"] = _slab_cols(f(inp["w_ffn2_gate"])[0])
    sh["wu2"] = _slab_cols(f(inp["w_ffn2_up"])[0])
    sh["wd2"] = _rows_part(f(inp["w_ffn2_down"])[0])
    win = f(inp["w_in"])[0]
    slabs = _slab_cols(win)
    wl = np.zeros((8, 128, 3 * 1024), np.float32)
    for pair in range(8):
        g, pp = pair // 4, pair % 4
        for m in range(3):
            wl[pair, :, m * 1024:(m + 1) * 1024] = slabs[g * 12 + m * 4 + pp]
    sh["win"] = wl
    sh["wout"] = _rows_part(f(inp["w_out"])[0])
    sh["wgate"] = _rows_part(f(inp["w_ple_gate"])[0])
    sh["wple"] = _rows_part(f(inp["w_ple_proj"])[0])
    return sh


def kernel(**inputs):
    x = np.asarray(inputs["x"], dtype=np.float32)
    p = np.asarray(inputs["p"], dtype=np.float32)
    sh = _prep_shared(inputs)
    if "nc" not in _NC_CACHE:
        _NC_CACHE["nc"] = build()
    nc = _NC_CACHE["nc"]
    n = 8
    in_maps = []
    for c in range(n):
        m = dict(sh)
        m["x"] = np.ascontiguousarray(x[c])
        m["p"] = np.ascontiguousarray(p[0, c])
        in_maps.append(m)
    res = run_bass_kernel_spmd(nc, in_maps, core_ids=list(range(n)))
    return np.stack([np.asarray(r["out"], dtype=np.float32) for r in res.results], 0)
```
